# Optimizing a Trainium2 kernel written in Bass

```python
import math
import jax, jax.numpy as jnp
from jax import lax
import numpy as np

D_MODEL = 2048
BATCH = 4
SEQ = 4096
DEPTH = 4

D_FF = 5632
DIFF_HEADS = 4
DIFF_HEAD_DIM = 64
MLA_HEADS = 6
MLA_Q_LORA = 512
MLA_KV_LORA = 512
MLA_NOPE = 128
MLA_ROPE = 64
MLA_V = 128
DIL_HEADS = 6
DIL_HEAD_DIM = 128
DILATED_PATTERNS = ((128, 1), (512, 4), (2048, 16))
DIL_BLOCK = 128
MEM_LEN = 256
CROSS_HEADS = 4
CROSS_HEAD_DIM = 128
Q_BLOCK = 128
ROPE_THETA = 10000.0
MAX_START = 1024
NEG = -1e30

A_COLS = 4 * DIFF_HEADS * DIFF_HEAD_DIM + DIFF_HEADS * 2 * DIFF_HEAD_DIM
B_COLS = MLA_Q_LORA + MLA_KV_LORA + MLA_ROPE
C_COLS = 3 * DIL_HEADS * DIL_HEAD_DIM
N_IN = A_COLS + B_COLS + C_COLS
MIX_OUT = DIFF_HEADS * 2 * DIFF_HEAD_DIM + MLA_HEADS * MLA_V + DIL_HEADS * DIL_HEAD_DIM

kernel_name = 'hybrid_parallel_headgroup_decoder'


def rmsnorm(x, g, eps=1e-6):
    xf = x.astype(jnp.float32)
    y = xf * lax.rsqrt(jnp.mean(xf * xf, axis=-1, keepdims=True) + eps)
    return (y * g.astype(jnp.float32)).astype(x.dtype)


def swiglu(x, w_gate, w_up, w_down):
    return (jax.nn.silu(x @ w_gate) * (x @ w_up)) @ w_down


def rope_tables(positions, dim):
    inv = 1.0 / (ROPE_THETA ** (jnp.arange(0, dim, 2, dtype=jnp.float32) / dim))
    ang = positions.astype(jnp.float32)[..., None] * inv
    return jnp.cos(ang), jnp.sin(ang)


def apply_rope(x, cos, sin):
    x1, x2 = jnp.split(x, 2, axis=-1)
    c = cos[:, :, None, :].astype(x.dtype)
    s = sin[:, :, None, :].astype(x.dtype)
    return jnp.concatenate([x1 * c - x2 * s, x1 * s + x2 * c], axis=-1)


def heads_first(a):
    return jnp.swapaxes(a, 1, 2)


def heads_last(a):
    B, H, S, D = a.shape
    return jnp.swapaxes(a, 1, 2).reshape(B, S, H * D)


def causal_attention(q, k, v, scale):
    B, H, S, Dk = q.shape
    Dv = v.shape[-1]
    nb = S // Q_BLOCK
    qb = q.reshape(B, H, nb, Q_BLOCK, Dk).transpose(2, 0, 1, 3, 4)
    kpos = jnp.arange(S)

    def one_block(args):
        qi, i = args
        s = jnp.einsum('bhqd,bhkd->bhqk', qi, k).astype(jnp.float32) * scale
        qpos = i * Q_BLOCK + jnp.arange(Q_BLOCK)
        s = jnp.where(kpos[None, :] <= qpos[:, None], s, NEG)
        p = jax.nn.softmax(s, axis=-1)
        return jnp.einsum('bhqk,bhkd->bhqd', p.astype(v.dtype), v)

    out = lax.map(one_block, (qb, jnp.arange(nb)))
    return out.transpose(1, 2, 0, 3, 4).reshape(B, H, S, Dv)


def dilated_window_attention(q, k, v, window, dilation, scale):
    B, H, S, D = q.shape
    Lb = DIL_BLOCK
    span = window // dilation
    unit = dilation * Lb
    Sp = -(-S // unit) * unit
    M = Sp // dilation
    nb = M // Lb

    def to_blocks(a):
        a = jnp.pad(a, ((0, 0), (0, 0), (0, Sp - S), (0, 0))).reshape(B, H, M, dilation, D)
        return jnp.swapaxes(a, 2, 3).reshape(B, H, dilation, nb, Lb, D)

    def with_prev(a):
        prev = jnp.pad(a, ((0, 0), (0, 0), (0, 0), (1, 0), (0, 0), (0, 0)))[:, :, :, :-1]
        return jnp.concatenate([prev, a], axis=4)

    qb = to_blocks(q)
    kk = with_prev(to_blocks(k))
    vv = with_prev(to_blocks(v))
    s = jnp.einsum('bhrnqd,bhrnkd->bhrnqk', qb, kk).astype(jnp.float32) * scale
    qi = jnp.arange(Lb)[:, None]
    kj = jnp.arange(2 * Lb)[None, :]
    dist = qi + Lb - kj
    blk = jnp.arange(nb)[:, None, None]
    valid = (dist >= 0) & (dist <= span) & (blk * Lb + kj - Lb >= 0)
    s = jnp.where(valid, s, NEG)
    m = jnp.max(s, axis=-1, keepdims=True)
    e = jnp.exp(s - m)
    l = jnp.sum(e, axis=-1, keepdims=True)
    o = jnp.einsum('bhrnqk,bhrnkd->bhrnqd', (e / l).astype(v.dtype), vv)
    lse = (m + jnp.log(l))[..., 0]

    def from_blocks(a):
        tail = a.shape[5:]
        a = a.reshape((B, H, dilation, M) + tail)
        a = jnp.swapaxes(a, 2, 3).reshape((B, H, Sp) + tail)
        return a[:, :, :S]

    return from_blocks(o), from_blocks(lse)


def diff_attention(u, lam_params, subln, lam_init, cos, sin):
    B, S, _ = u.shape
    n = DIFF_HEADS * DIFF_HEAD_DIM
    q1, q2, k1, k2, v = jnp.split(u, [n, 2 * n, 3 * n, 4 * n], axis=-1)
    shp = (B, S, DIFF_HEADS, DIFF_HEAD_DIM)
    q1, q2, k1, k2 = [heads_first(apply_rope(t.reshape(shp), cos, sin)) for t in (q1, q2, k1, k2)]
    vh = heads_first(v.reshape(B, S, DIFF_HEADS, 2 * DIFF_HEAD_DIM))
    scale = DIFF_HEAD_DIM ** -0.5
    a1 = causal_attention(q1, k1, vh, scale)
    a2 = causal_attention(q2, k2, vh, scale)
    lp = lam_params.astype(jnp.float32)
    lam = jnp.exp(jnp.sum(lp[0] * lp[1])) - jnp.exp(jnp.sum(lp[2] * lp[3])) + lam_init
    o = jnp.swapaxes(a1 - lam.astype(a1.dtype) * a2, 1, 2)
    o = rmsnorm(o, subln, eps=1e-5) * (1.0 - lam_init)
    return o.reshape(B, S, -1)


def mla_attention(u, q_norm, w_uq, kv_norm, w_ukv, cos, sin):
    B, S, _ = u.shape
    c_q, c_kv, k_rope = jnp.split(u, [MLA_Q_LORA, MLA_Q_LORA + MLA_KV_LORA], axis=-1)
    q = (rmsnorm(c_q, q_norm) @ w_uq).reshape(B, S, MLA_HEADS, MLA_NOPE + MLA_ROPE)
    q_nope, q_rope = jnp.split(q, [MLA_NOPE], axis=-1)
    kv = (rmsnorm(c_kv, kv_norm) @ w_ukv).reshape(B, S, MLA_HEADS, MLA_NOPE + MLA_V)
    k_nope, v = jnp.split(kv, [MLA_NOPE], axis=-1)
    q_rope = apply_rope(q_rope, cos, sin)
    k_rope = apply_rope(k_rope[:, :, None, :], cos, sin)
    q = jnp.concatenate([q_nope, q_rope], axis=-1)
    k = jnp.concatenate([k_nope, jnp.broadcast_to(k_rope, (B, S, MLA_HEADS, MLA_ROPE))], axis=-1)
    o = causal_attention(heads_first(q), heads_first(k), heads_first(v), (MLA_NOPE + MLA_ROPE) ** -0.5)
    return heads_last(o)


def dilated_attention(u, cos, sin):
    B, S, _ = u.shape
    shp = (B, S, DIL_HEADS, DIL_HEAD_DIM)
    q, k, v = [t.reshape(shp) for t in jnp.split(u, 3, axis=-1)]
    q = heads_first(apply_rope(q, cos, sin))
    k = heads_first(apply_rope(k, cos, sin))
    v = heads_first(v)
    outs, lses = [], []
    for window, dilation in DILATED_PATTERNS:
        o, lse = dilated_window_attention(q, k, v, window, dilation, DIL_HEAD_DIM ** -0.5)
        outs.append(o)
        lses.append(lse)
    wts = jax.nn.softmax(jnp.stack(lses, axis=0), axis=0)
    o = jnp.einsum('gbhs,gbhsd->bhsd', wts.astype(v.dtype), jnp.stack(outs, axis=0))
    return heads_last(o)


def cross_attention(h, memn, wq, wkv, wo):
    B, S, _ = h.shape
    Mm = memn.shape[1]
    q = (h @ wq).reshape(B, S, CROSS_HEADS, CROSS_HEAD_DIM)
    kv = (memn @ wkv).reshape(B, Mm, 2, CROSS_HEADS, CROSS_HEAD_DIM)
    k, v = kv[:, :, 0], kv[:, :, 1]
    s = jnp.einsum('bshd,bmhd->bhsm', q, k).astype(jnp.float32) * (CROSS_HEAD_DIM ** -0.5)
    p = jax.nn.softmax(s, axis=-1)
    o = jnp.einsum('bhsm,bmhd->bshd', p.astype(v.dtype), v).reshape(B, S, CROSS_HEADS * CROSS_HEAD_DIM)
    return o @ wo


def setup_inputs(seed: int = 0) -> dict:
    key = jax.random.key(seed)
    ks = jax.random.split(key, 26)
    L, D, F = DEPTH, D_MODEL, D_FF
    f32 = jnp.float32

    def w(k, shape, fan_in):
        return jax.random.normal(k, shape, f32) * (fan_in ** -0.5)

    def gain(k, shape):
        return 1.0 + 0.02 * jax.random.normal(k, shape, f32)

    x = jax.random.normal(ks[0], (BATCH, SEQ, D), f32)
    mem = jax.random.normal(ks[1], (BATCH, MEM_LEN, D), f32)
    start = jax.random.randint(ks[2], (BATCH, 1), 0, MAX_START, dtype=jnp.int32)
    positions = start + jnp.arange(SEQ, dtype=jnp.int32)[None, :]
    return {
        'x': x,
        'mem': mem,
        'positions': positions,
        'ffn1_norm': gain(ks[3], (L, D)),
        'ffn1_gate': w(ks[4], (L, D, F), D),
        'ffn1_up': w(ks[5], (L, D, F), D),
        'ffn1_down': w(ks[6], (L, F, D), F),
        'mix_norm': gain(ks[7], (L, D)),
        'w_in': w(ks[8], (L, D, N_IN), D),
        'diff_lambda': 0.1 * jax.random.normal(ks[9], (L, 4, DIFF_HEAD_DIM), f32),
        'diff_subln': gain(ks[10], (L, 2 * DIFF_HEAD_DIM)),
        'mla_q_norm': gain(ks[11], (L, MLA_Q_LORA)),
        'mla_w_uq': w(ks[12], (L, MLA_Q_LORA, MLA_HEADS * (MLA_NOPE + MLA_ROPE)), MLA_Q_LORA),
        'mla_kv_norm': gain(ks[13], (L, MLA_KV_LORA)),
        'mla_w_ukv': w(ks[14], (L, MLA_KV_LORA, MLA_HEADS * (MLA_NOPE + MLA_V)), MLA_KV_LORA),
        'w_out': w(ks[15], (L, MIX_OUT, D), MIX_OUT),
        'cross_norm': gain(ks[16], (L, D)),
        'mem_norm': gain(ks[17], (L, D)),
        'cross_wq': w(ks[18], (L, D, CROSS_HEADS * CROSS_HEAD_DIM), D),
        'cross_wkv': w(ks[19], (L, D, 2 * CROSS_HEADS * CROSS_HEAD_DIM), D),
        'cross_wo': w(ks[20], (L, CROSS_HEADS * CROSS_HEAD_DIM, D), CROSS_HEADS * CROSS_HEAD_DIM),
        'ffn2_norm': gain(ks[21], (L, D)),
        'ffn2_gate': w(ks[22], (L, D, F), D),
        'ffn2_up': w(ks[23], (L, D, F), D),
        'ffn2_down': w(ks[24], (L, F, D), F),
        'final_norm': gain(ks[25], (D,)),
    }


def reference(x, mem, positions, ffn1_norm, ffn1_gate, ffn1_up, ffn1_down, mix_norm, w_in,
              diff_lambda, diff_subln, mla_q_norm, mla_w_uq, mla_kv_norm, mla_w_ukv, w_out,
              cross_norm, mem_norm, cross_wq, cross_wkv, cross_wo,
              ffn2_norm, ffn2_gate, ffn2_up, ffn2_down, final_norm):
    cos64, sin64 = rope_tables(positions, DIFF_HEAD_DIM)
    cos128, sin128 = rope_tables(positions, DIL_HEAD_DIM)
    h = x
    for l in range(DEPTH):
        lam_init = 0.8 - 0.6 * math.exp(-0.3 * l)
        h = h + 0.5 * swiglu(rmsnorm(h, ffn1_norm[l]), ffn1_gate[l], ffn1_up[l], ffn1_down[l])
        u = rmsnorm(h, mix_norm[l]) @ w_in[l]
        u_a, u_b, u_c = jnp.split(u, [A_COLS, A_COLS + B_COLS], axis=-1)
        y_a = diff_attention(u_a, diff_lambda[l], diff_subln[l], lam_init, cos64, sin64)
        y_b = mla_attention(u_b, mla_q_norm[l], mla_w_uq[l], mla_kv_norm[l], mla_w_ukv[l], cos64, sin64)
        y_c = dilated_attention(u_c, cos128, sin128)
        h = h + jnp.concatenate([y_a, y_b, y_c], axis=-1) @ w_out[l]
        h = h + cross_attention(rmsnorm(h, cross_norm[l]), rmsnorm(mem, mem_norm[l]),
                                cross_wq[l], cross_wkv[l], cross_wo[l])
        h = h + 0.5 * swiglu(rmsnorm(h, ffn2_norm[l]), ffn2_gate[l], ffn2_up[l], ffn2_down[l])
    return rmsnorm(h, final_norm)
```

```python
import math
from contextlib import ExitStack
import numpy as np
import concourse.bass as bass
import concourse.mybir as mybir
from concourse.bass_utils import run_bass_kernel_spmd

F32, BF16, I32 = mybir.dt.float32, mybir.dt.bfloat16, mybir.dt.int32
AF = mybir.ActivationFunctionType
ALU = mybir.AluOpType

D = 2048
FF = 5632
L = 4
T = 2048
KC = D // 128
FC = FF // 128
NQT = 4
GT = ((0, 3, 4, 7), (1, 2, 5, 6))
OWNER = {}
for _p in (0, 1):
    for _j, _g in enumerate(GT[_p]):
        OWNER[_g] = (_p, _j)
KF_ROWS = 2112
KOFF_A, KOFF_BN, KOFF_BR, KOFF_C = 0, 512, 1280, 1344
VOFF_A, VOFF_B, VOFF_C = 0, 512, 1280
DIL_TILES = ((0, 2), (0, 4), (0, 6), (2, 8))
WSLOT = 4096
NWS = 4
VC_FFN1, VC_MIX, VC_CROSS, VC_FFN2, VC_MEM = 0, 16, 32, 48, 64
VC_QN, VC_KVN, VC_SUBLN, VC_LAM, VC_LCST, VC_FINAL = 80, 84, 88, 89, 345, 347
NVC = 363


TRACE = None


def check_trace(tr):
    pos = {e: 0 for e in tr}
    sems = {}
    prog = True
    while prog:
        prog = False
        for e, lst in tr.items():
            while pos[e] < len(lst):
                kind, sem, v = lst[pos[e]]
                if kind == "w":
                    if sems.get(sem, 0) < v:
                        break
                else:
                    sems[sem] = sems.get(sem, 0) + v
                pos[e] += 1
                prog = True
    stuck = {e: (pos[e], len(lst), lst[pos[e]], sems.get(lst[pos[e]][1], 0)) for e, lst in tr.items()
             if pos[e] < len(lst)}
    return stuck


class Buf:
    __slots__ = ("name", "w", "r", "dsem", "dcnt")

    def __init__(self, name):
        self.name = name
        self.w = []
        self.r = []
        self.dsem = None
        self.dcnt = 0


class Eng:
    def __init__(self, name, obj, sem):
        self.name, self.obj, self.sem = name, obj, sem
        self.cnt = 0
        self.seen = {}


def _merge(lst, ev):
    out = [e for e in lst if e[0] is not ev[0]]
    out.append(ev)
    return out


class KB:
    def __init__(self, nc):
        self.nc = nc
        self.pe = Eng("pe", nc.tensor, nc.alloc_semaphore("s_pe"))
        self.act = Eng("act", nc.scalar, nc.alloc_semaphore("s_act"))
        self.dve = Eng("dve", nc.vector, nc.alloc_semaphore("s_dve"))
        self.pool = Eng("pool", nc.gpsimd, nc.alloc_semaphore("s_pool"))
        self.sp = Eng("sp", nc.sync, nc.alloc_semaphore("s_sp"))
        self.engs = [self.pe, self.act, self.dve, self.pool, self.sp]
        self.dsems = []
        self.nbuf = 0

    def buf(self, name="b"):
        self.nbuf += 1
        return Buf(f"{name}{self.nbuf}")

    def _wait(self, E, evs):
        for (sem, val) in evs:
            if E is self.pe and sem is self.pe.sem:
                continue
            if E.seen.get(sem, 0) < val:
                E.obj.wait_ge(sem, val)
                E.seen[sem] = val
                if TRACE is not None:
                    TRACE.setdefault(E.name, []).append(("w", id(sem), val))

    def op(self, E, fn, reads=(), writes=(), signal=True):
        evs = []
        for b in reads:
            evs += b.w
        for b in writes:
            evs += b.w
            evs += b.r
        self._wait(E, evs)
        ins = fn()
        if signal:
            E.cnt += 1
            ins.then_inc(E.sem, 1)
            ev = (E.sem, E.cnt)
            if TRACE is not None:
                TRACE.setdefault(E.name, []).append(("i", id(E.sem), 1))
        else:
            ev = (E.sem, E.cnt + 1)
        for b in reads:
            b.r = _merge(b.r, ev)
        for b in writes:
            b.w = _merge(b.w, ev)
            b.r = []
        return ins

    def dma(self, Q, out, in_, reads=(), writes=(), tag=None):
        if tag.dsem is None:
            tag.dsem = self.nc.alloc_semaphore("d_" + tag.name)
            self.dsems.append(tag)
        evs = []
        if tag.dcnt:
            evs.append((tag.dsem, tag.dcnt))
        for b in reads:
            evs += b.w
        for b in writes:
            evs += b.w
            evs += b.r
        self._wait(Q, evs)
        tag.dcnt += 16
        Q.obj.dma_start(out=out, in_=in_).then_inc(tag.dsem, 16)
        if TRACE is not None:
            TRACE.setdefault(Q.name, []).append(("i", id(tag.dsem), 16))
        ev = (tag.dsem, tag.dcnt)
        for b in reads:
            b.r = _merge(b.r, ev)
        for b in writes:
            b.w = _merge(b.w, ev)
            b.r = []

    def barrier(self, engines=None):
        evs = [(e.sem, e.cnt) for e in self.engs if e.cnt]
        evs += [(b.dsem, b.dcnt) for b in self.dsems if b.dcnt]
        for E in (engines or self.engs):
            self._wait(E, evs)


class Prog:
    def __init__(self, mode):
        self.mode = mode
        self.nc = bass.Bass("TRN2", target_bir_lowering=False)
        self.k = KB(self.nc)
        self.st = ExitStack()
        self.dram = {}
        self.dbuf = {}
        nc = self.nc
        self.ps = self.st.enter_context(nc.psum_tensor("ps", [128, 8, 512], F32))
        self.psb = [self.k.buf("psum") for _ in range(8)]
        self.wring = self.st.enter_context(nc.sbuf_tensor("sb_wring", [128, NWS, WSLOT], BF16))
        self.wbuf = [self.k.buf("w") for _ in range(NWS)]
        self.wq = []
        self.wissued = 0
        self.wcons = 0
        self.cst = self.st.enter_context(nc.sbuf_tensor("sb_cst", [128, 8], F32))
        self.ident = self.st.enter_context(nc.sbuf_tensor("sb_ident", [128, 128], F32))
        self.perm64 = self.st.enter_context(nc.sbuf_tensor("sb_perm64", [128, 128], BF16))
        self.perm128 = self.st.enter_context(nc.sbuf_tensor("sb_perm128", [128, 128], BF16))
        self.ones = self.st.enter_context(nc.sbuf_tensor("sb_ones", [128, 128], BF16))
        self.identb = self.st.enter_context(nc.sbuf_tensor("sb_identb", [128, 128], BF16))
        self.vec = self.st.enter_context(nc.sbuf_tensor("sb_vec", [128, NVC], F32))
        self.cbuf = self.k.buf("const")
        self.vbuf = self.k.buf("vec")

    def din(self, name, shape, dtype=F32):
        t = self.nc.dram_tensor(name, list(shape), dtype, kind="ExternalInput").ap()
        self.dram[name] = t
        self.dbuf[name] = self.k.buf(name)
        return t

    def dout(self, name, shape, dtype=F32):
        t = self.nc.dram_tensor(name, list(shape), dtype, kind="ExternalOutput").ap()
        self.dram[name] = t
        self.dbuf[name] = self.k.buf(name)
        return t

    def dint(self, name, shape, dtype=F32):
        t = self.nc.dram_tensor(name, list(shape), dtype, kind="Internal").ap()
        self.dram[name] = t
        self.dbuf[name] = self.k.buf(name)
        return t

    def wpush(self, src, kcn, ncols):
        assert kcn * ncols <= WSLOT
        self.wq.append((src, kcn, ncols))

    def _wissue(self):
        i = self.wissued
        src, kcn, ncols = self.wq[i]
        s = i % NWS
        dst = self.wring[:, s, 0:kcn * ncols].rearrange("p (c n) -> p c n", n=ncols)
        self.k.dma(self.k.pool, dst, src.rearrange("(c p) n -> p c n", p=128),
                   writes=[self.wbuf[s]], tag=self.wbuf[s])
        self.wissued += 1

    def wnext(self):
        i = self.wcons
        while self.wissued < len(self.wq) and self.wissued < i + NWS - 1:
            self._wissue()
        assert self.wissued > i
        src, kcn, ncols = self.wq[i]
        s = i % NWS
        self.wcons += 1
        view = self.wring[:, s, 0:kcn * ncols].rearrange("p (c n) -> p c n", n=ncols)
        return view, self.wbuf[s]

    def load_consts(self):
        k, nc = self.k, self.nc
        cst = self.din("cst", [128, 8])
        ident = self.din("ident", [128, 128])
        p64 = self.din("perm64", [128, 128])
        p128 = self.din("perm128", [128, 128])
        k.dma(k.sp, self.cst[:], cst[:, :], writes=[self.cbuf], tag=self.cbuf)
        k.dma(k.sp, self.ident[:], ident[:, :], writes=[self.cbuf], tag=self.cbuf)
        ptag = k.buf("constp")
        k.dma(k.pool, self.perm64[:], p64[:, :], writes=[self.cbuf], tag=ptag)
        k.dma(k.pool, self.perm128[:], p128[:, :], writes=[self.cbuf], tag=ptag)
        k.dma(k.pool, self.identb[:], ident[:, :], writes=[self.cbuf], tag=ptag)
        k.op(k.dve, lambda: nc.vector.memset(self.ones[:], 1.0), writes=[self.cbuf])

    def load_vec(self, src):
        k = self.k
        k.dma(k.sp, self.vec[:], src, writes=[self.vbuf], tag=self.vbuf)

    def bank(self, i):
        return self.ps[:, i, :], self.psb[i]


def rms_alloc(P, st, ncol, name):
    k, nc = P.k, P.nc
    R = {}
    R["sq"] = st.enter_context(nc.sbuf_tensor(name + "_sq", [128, 2, ncol], BF16))
    R["sqb"] = [k.buf("sq"), k.buf("sq")]
    R["rstd"] = st.enter_context(nc.sbuf_tensor(name + "_rstd", [128, ncol], F32))
    R["rb"] = k.buf("rstd")
    R["ncol"] = ncol
    return R


def rms_rstd(P, R, chunks, nch, dim, eps, bank_ids):
    k, nc = P.k, P.nc
    sq, sqb, rstd, rb, ncol = R["sq"], R["sqb"], R["rstd"], R["rb"], R["ncol"]
    nh = max(1, ncol // 512)
    w = min(ncol, 512)
    for c in range(nch):
        ap, b = chunks(c)
        s = c % 2
        k.op(k.act, lambda ap=ap, s=s: nc.scalar.activation(out=sq[:, s, :], in_=ap, func=AF.Square),
             reads=[b], writes=[sqb[s]])
        for h in range(nh):
            pa, pb = P.bank(bank_ids[h])
            k.op(k.pe, lambda pa=pa, s=s, h=h, c=c: nc.tensor.matmul(
                pa[:, 0:w], P.ones[:], sq[:, s, h * w:(h + 1) * w], start=(c == 0), stop=(c == nch - 1)),
                reads=[sqb[s], P.cbuf], writes=[pb])
    ecol = 5 if eps == 1e-6 else 6
    for h in range(nh):
        pa, pb = P.bank(bank_ids[h])
        k.op(k.act, lambda pa=pa, h=h: nc.scalar.activation(
            out=rstd[:, h * w:(h + 1) * w], in_=pa[:, 0:w], func=AF.Sqrt, bias=P.cst[:, ecol:ecol + 1],
            scale=1.0 / dim), reads=[pb, P.cbuf], writes=[rb])
    k.op(k.dve, lambda: nc.vector.reciprocal(out=rstd[:], in_=rstd[:]), reads=[rb], writes=[rb])
    return rstd, rb


def phase_x(P, x_in, pos_in, hT, ropeT):
    k, nc = P.k, P.nc
    with ExitStack() as st:
        xin = st.enter_context(nc.sbuf_tensor("xin", [128, 2, D], F32))
        xb = [k.buf("xin"), k.buf("xin")]
        xo = st.enter_context(nc.sbuf_tensor("xo", [128, 4, 512], F32))
        xob = [k.buf("xo") for _ in range(4)]
        posi = st.enter_context(nc.sbuf_tensor("posi", [128, T], I32))
        posf = st.enter_context(nc.sbuf_tensor("posf", [128, T], F32))
        ang = st.enter_context(nc.sbuf_tensor("ang", [128, T], F32))
        tab = st.enter_context(nc.sbuf_tensor("tab", [128, T], F32))
        pb_, fb_, ab_, tb_ = k.buf("posi"), k.buf("posf"), k.buf("ang"), k.buf("tab")
        k.dma(k.sp, posi[:], pos_in.partition_broadcast(128), writes=[pb_], tag=pb_)
        k.op(k.dve, lambda: nc.vector.tensor_copy(out=posf[:], in_=posi[:]), reads=[pb_], writes=[fb_])
        for ti in range(4):
            inv_col = 0 if ti < 2 else 2
            sgn_col = 1 if ti < 2 else 3
            is_cos = (ti % 2 == 0)
            k.op(k.dve, lambda inv_col=inv_col: nc.vector.tensor_scalar(
                out=ang[:], in0=posf[:], scalar1=P.cst[:, inv_col:inv_col + 1], scalar2=None, op0=ALU.mult),
                reads=[fb_, P.cbuf], writes=[ab_])
            C1 = 6.28125
            C2 = 2.0 * math.pi - C1
            if is_cos:
                k.op(k.dve, lambda: nc.vector.tensor_scalar(
                    out=ang[:], in0=ang[:], scalar1=math.pi / 2, scalar2=None, op0=ALU.add),
                    reads=[ab_], writes=[ab_])
            k.op(k.dve, lambda: nc.vector.tensor_scalar(
                out=posi[:], in0=ang[:], scalar1=1.0 / (2 * math.pi), scalar2=None, op0=ALU.mult),
                reads=[ab_], writes=[pb_])
            k.op(k.dve, lambda: nc.vector.tensor_copy(out=tab[:], in_=posi[:]), reads=[pb_], writes=[tb_])
            k.op(k.dve, lambda: nc.vector.scalar_tensor_tensor(
                out=ang[:], in0=tab[:], scalar=-C1, in1=ang[:], op0=ALU.mult, op1=ALU.add),
                reads=[tb_, ab_], writes=[ab_])
            k.op(k.dve, lambda: nc.vector.scalar_tensor_tensor(
                out=ang[:], in0=tab[:], scalar=-C2, in1=ang[:], op0=ALU.mult, op1=ALU.add),
                reads=[tb_, ab_], writes=[ab_])
            k.op(k.dve, lambda: nc.vector.tensor_scalar(
                out=ang[:], in0=ang[:], scalar1=-3.1415925, scalar2=3.1415925, op0=ALU.max, op1=ALU.min),
                reads=[ab_], writes=[ab_])
            k.op(k.act, lambda: nc.scalar.activation(out=tab[:], in_=ang[:], func=AF.Sin),
                 reads=[ab_], writes=[tb_])
            if not is_cos:
                k.op(k.dve, lambda sgn_col=sgn_col: nc.vector.tensor_scalar(
                    out=tab[:], in0=tab[:], scalar1=P.cst[:, sgn_col:sgn_col + 1], scalar2=None, op0=ALU.mult),
                    reads=[tb_, P.cbuf], writes=[tb_])
            k.dma(k.sp, ropeT[ti, :, :], tab[:], reads=[tb_], writes=[P.dbuf["ropeT"]], tag=tb_)
        nb = T // 128
        for tb in range(nb):
            s = tb % 2
            k.dma(k.sp, xin[:, s, :], x_in[tb * 128:(tb + 1) * 128, :], writes=[xb[s]], tag=xb[s])
            for cg in range(4):
                bi = (tb * 4 + cg) % 8
                pa, pb = P.bank(bi)
                for ci in range(4):
                    c = cg * 4 + ci
                    k.op(k.pe, lambda pa=pa, ci=ci, c=c, s=s: nc.tensor.transpose(
                        pa[:, ci * 128:(ci + 1) * 128], xin[:, s, c * 128:(c + 1) * 128], P.ident[:]),
                        reads=[xb[s], P.cbuf], writes=[pb], signal=(ci == 3))
                o = (tb * 4 + cg) % 4
                k.op(k.dve, lambda pa=pa, o=o: nc.vector.tensor_copy(out=xo[:, o, :], in_=pa),
                     reads=[pb], writes=[xob[o]])
                dst = hT[cg * 512:(cg + 1) * 512, tb * 128:(tb + 1) * 128].rearrange("(c p) t -> p c t", p=128)
                k.dma(k.sp, dst, xo[:, o, :].rearrange("p (c t) -> p c t", t=128),
                      reads=[xob[o]], writes=[P.dbuf["hT"]], tag=xob[o])
        k.barrier()


def phase_z(P, hT, out, gcol):
    k, nc = P.k, P.nc
    with ExitStack() as st:
        hS = st.enter_context(nc.sbuf_tensor("z_h", [128, KC, 512], F32))
        hb = [k.buf("zh") for _ in range(KC)]
        yo = st.enter_context(nc.sbuf_tensor("z_o", [128, 2, D], F32))
        yb = [k.buf("zo"), k.buf("zo")]
        R = rms_alloc(P, st, 512, "z")
        for tt in range(T // 512):
            for c in range(KC):
                k.dma(k.sp, hS[:, c, :], hT[c * 128:(c + 1) * 128, tt * 512:(tt + 1) * 512],
                      reads=[P.dbuf["hT"]], writes=[hb[c]], tag=hb[c])
            if True:
                rstd, rb = rms_rstd(P, R, lambda c: (hS[:, c, :], hb[c]), KC, D, 1e-6, [0])
                for c in range(KC):
                    k.op(k.dve, lambda c=c: nc.vector.scalar_tensor_tensor(
                        out=hS[:, c, :], in0=hS[:, c, :], scalar=P.vec[:, gcol + c:gcol + c + 1], in1=rstd[:],
                        op0=ALU.mult, op1=ALU.mult), reads=[hb[c], rb, P.vbuf], writes=[hb[c]])
                for tb in range(4):
                    s = tb % 2
                    for cg in range(4):
                        bi = 4 + (tb * 4 + cg) % 4
                        pa, pb = P.bank(bi)
                        for ci in range(4):
                            c = cg * 4 + ci
                            k.op(k.pe, lambda pa=pa, ci=ci, c=c, tb=tb: nc.tensor.transpose(
                                pa[:, ci * 128:(ci + 1) * 128], hS[:, c, tb * 128:(tb + 1) * 128], P.ident[:]),
                                reads=[hb[c], P.cbuf], writes=[pb], signal=(ci == 3))
                        k.op(k.act, lambda pa=pa, s=s, cg=cg: nc.scalar.copy(
                            out=yo[:, s, cg * 512:(cg + 1) * 512], in_=pa), reads=[pb], writes=[yb[s]])
                    r0 = tt * 512 + tb * 128
                    k.dma(k.sp, out[r0:r0 + 128, :], yo[:, s, :], reads=[yb[s]], writes=[P.dbuf["out"]],
                          tag=yb[s])
        k.barrier()


def phase_ffn(P, hT, gcol, Wg, Wu, Wd, TT=512):
    k, nc = P.k, P.nc
    NH = TT // 512
    with ExitStack() as st:
        xn = st.enter_context(nc.sbuf_tensor("f_xn", [128, KC, TT], BF16))
        xnb = k.buf("xn")
        HT = st.enter_context(nc.sbuf_tensor("f_ht", [128, FC, TT], BF16))
        htb = [k.buf("ht") for _ in range(FC)]
        hbuf = st.enter_context(nc.sbuf_tensor("f_h", [128, 3, TT], F32))
        hbb = [k.buf("fh") for _ in range(3)]
        sg = st.enter_context(nc.sbuf_tensor("f_sg", [128, 2, 512], F32))
        sgb = [k.buf("sg"), k.buf("sg")]
        ob = st.enter_context(nc.sbuf_tensor("f_o", [128, 3, 512], F32))
        obb = [k.buf("fo") for _ in range(3)]
        hTb = P.dbuf["hT"]
        R = rms_alloc(P, st, TT, "f")
        for tt in range(T // TT):
            t0 = tt * TT
            for fg in range(FF // 256):
                P.wpush(Wg[:, fg * 256:(fg + 1) * 256], KC, 256)
                P.wpush(Wu[:, fg * 256:(fg + 1) * 256], KC, 256)
            fblocks = [(0, 16), (16, 16), (32, 12)]
            for dg in range(D // 256):
                for (f0, fn_) in fblocks:
                    P.wpush(Wd[f0 * 128:(f0 + fn_) * 128, dg * 256:(dg + 1) * 256], fn_, 256)
            hcnt = [0]

            def hload(c):
                s = hcnt[0] % 3
                hcnt[0] += 1
                k.dma(k.sp, hbuf[:, s, :], hT[c * 128:(c + 1) * 128, t0:t0 + TT], reads=[hTb],
                      writes=[hbb[s]], tag=hbb[s])
                return hbuf[:, s, :], hbb[s]

            if True:
                rstd, rb = rms_rstd(P, R, hload, KC, D, 1e-6, [6, 7])
                for c in range(KC):
                    ap, b = hload(c)
                    k.op(k.dve, lambda ap=ap, c=c: nc.vector.scalar_tensor_tensor(
                        out=xn[:, c, :], in0=ap, scalar=P.vec[:, gcol + c:gcol + c + 1], in1=rstd[:],
                        op0=ALU.mult, op1=ALU.mult), reads=[b, rb, P.vbuf], writes=[xnb])
            it = 0
            for fg in range(FF // 256):
                wg, wgb = P.wnext()
                wu, wub = P.wnext()
                for sub in range(2):
                    fc = fg * 2 + sub
                    for h in range(NH):
                        ga, gb = P.bank(it % 2)
                        ua, ub = P.bank(2 + it % 2)
                        for c in range(KC):
                            k.op(k.pe, lambda ga=ga, c=c, sub=sub, h=h, wg=wg: nc.tensor.matmul(
                                ga, wg[:, c, sub * 128:(sub + 1) * 128], xn[:, c, h * 512:(h + 1) * 512],
                                start=(c == 0), stop=(c == KC - 1)),
                                reads=[wgb, xnb], writes=[gb], signal=(c == KC - 1))
                        for c in range(KC):
                            k.op(k.pe, lambda ua=ua, c=c, sub=sub, h=h, wu=wu: nc.tensor.matmul(
                                ua, wu[:, c, sub * 128:(sub + 1) * 128], xn[:, c, h * 512:(h + 1) * 512],
                                start=(c == 0), stop=(c == KC - 1)),
                                reads=[wub, xnb], writes=[ub], signal=(c == KC - 1))
                        s = it % 2
                        k.op(k.act, lambda ga=ga, s=s: nc.scalar.activation(out=sg[:, s, :], in_=ga, func=AF.Silu),
                             reads=[gb], writes=[sgb[s]])
                        k.op(k.dve, lambda ua=ua, s=s, fc=fc, h=h: nc.vector.tensor_tensor(
                            out=HT[:, fc, h * 512:(h + 1) * 512], in0=ua, in1=sg[:, s, :], op=ALU.mult),
                            reads=[ub, sgb[s]], writes=[htb[fc]])
                        it += 1
            oc = 0
            for dg in range(D // 256):
                base = 4 if dg % 2 == 0 else 0
                wts = []
                for bi_, (f0, fn_) in enumerate(fblocks):
                    wd, wdb = P.wnext()
                    for j in range(fn_):
                        fc = f0 + j
                        for sub in range(2):
                            for h in range(NH):
                                pa, pb = P.bank(base + sub * NH + h)
                                k.op(k.pe, lambda pa=pa, wd=wd, j=j, sub=sub, fc=fc, h=h: nc.tensor.matmul(
                                    pa, wd[:, j, sub * 128:(sub + 1) * 128], HT[:, fc, h * 512:(h + 1) * 512],
                                    start=(fc == 0), stop=(fc == FC - 1)),
                                    reads=[wdb, htb[fc]], writes=[pb], signal=(fc == FC - 1))
                for sub in range(2):
                    dc = dg * 2 + sub
                    ap, b = hload(dc)
                    for h in range(NH):
                        pa, pb = P.bank(base + sub * NH + h)
                        s = oc % 3
                        oc += 1
                        k.op(k.dve, lambda pa=pa, ap=ap, s=s, h=h: nc.vector.scalar_tensor_tensor(
                            out=ob[:, s, :], in0=pa, scalar=0.5, in1=ap[:, h * 512:(h + 1) * 512],
                            op0=ALU.mult, op1=ALU.add), reads=[pb, b], writes=[obb[s]])
                        k.dma(k.sp, hT[dc * 128:(dc + 1) * 128, t0 + h * 512:t0 + (h + 1) * 512], ob[:, s, :],
                              reads=[obb[s]], writes=[hTb], tag=obb[s])
            k.barrier()


def _consts():
    p = np.arange(128)
    cst = np.zeros((128, 8), np.float32)
    inv64 = 1.0 / (10000.0 ** (np.arange(0, 64, 2, dtype=np.float32) / 64))
    inv128 = 1.0 / (10000.0 ** (np.arange(0, 128, 2, dtype=np.float32) / 128))
    cst[:, 0] = inv64[p % 32]
    cst[:, 1] = np.where((p % 64) < 32, -1.0, 1.0)
    cst[:, 2] = inv128[p % 64]
    cst[:, 3] = np.where(p < 64, -1.0, 1.0)
    cst[:, 5] = 1e-6
    cst[:, 6] = 1e-5
    ident = np.eye(128, dtype=np.float32)
    perm64 = np.zeros((128, 128), np.float32)
    perm128 = np.zeros((128, 128), np.float32)
    for m in range(128):
        perm64[(m // 64) * 64 + ((m % 64) + 32) % 64, m] = 1.0
        perm128[(m + 64) % 128, m] = 1.0
    return dict(cst=cst, ident=ident, perm64=perm64, perm128=perm128)


def _vec_pack(inp, l):
    v = np.zeros((128, NVC), np.float32)

    def col16(a):
        return np.ascontiguousarray(a.reshape(-1, 128).T)
    v[:, VC_FFN1:VC_FFN1 + 16] = col16(inp["ffn1_norm"][l])
    v[:, VC_MIX:VC_MIX + 16] = col16(inp["mix_norm"][l])
    v[:, VC_CROSS:VC_CROSS + 16] = col16(inp["cross_norm"][l])
    v[:, VC_FFN2:VC_FFN2 + 16] = col16(inp["ffn2_norm"][l])
    v[:, VC_MEM:VC_MEM + 16] = col16(inp["mem_norm"][l])
    v[:, VC_QN:VC_QN + 4] = col16(inp["mla_q_norm"][l])
    v[:, VC_KVN:VC_KVN + 4] = col16(inp["mla_kv_norm"][l])
    v[:, VC_SUBLN] = inp["diff_subln"][l]
    v[:, VC_LAM:VC_LAM + 256] = inp["diff_lambda"][l].reshape(1, 256)
    lam_init = 0.8 - 0.6 * math.exp(-0.3 * l)
    v[:, VC_LCST] = -lam_init
    v[:, VC_LCST + 1] = 1.0 - lam_init
    v[:, VC_FINAL:VC_FINAL + 16] = col16(inp["final_norm"])
    return v


def _core_tokens(a, b, par):
    return np.ascontiguousarray(np.concatenate([a[b, g * 512:(g + 1) * 512] for g in GT[par]], axis=0))


_PROGS = {}
import os
DBG = os.environ.get('KDBG', '')
A_W = dict(ffn1_gate=(D, FF), ffn1_up=(D, FF), ffn1_down=(FF, D), w_r64=(D, 1024), w_r128=(D, 1536), w_kr=(D, 64),
           w_c=(D, 1024), w_v=(D, 1280), w_uq=(512, 1152), w_ukv=(512, 1536))
B_W = dict(w_out=(D, D), cross_wq=(D, 512), cross_wkv=(D, 1024), cross_wo=(512, D), ffn2_gate=(D, FF),
           ffn2_up=(D, FF), ffn2_down=(FF, D))


def _copy_h(P, hin, hT):
    k = P.k
    for c in range(KC):
        b = k.buf("hcp")
        k.dma(k.sp, hT[c * 128:(c + 1) * 128, :], hin[c * 128:(c + 1) * 128, :], writes=[P.dbuf["hT"]], tag=b)


def _get_prog(name):
    if name in _PROGS:
        return _PROGS[name]
    P = Prog(name)
    P.load_consts()
    if name == "X":
        x = P.din("x", [T, D])
        pos = P.din("pos", [T], I32)
        mem = P.din("mem", [256, D])
        hT = P.dout("hT", [D, T])
        ropeT = P.dout("ropeT", [4, 128, T])
        memT = P.dout("memT", [D, 256])
        phase_x(P, x, pos, hT, ropeT)
        phase_mem(P, mem, memT)
    elif name == "A":
        vec = P.din("vec", [128, NVC])
        P.load_vec(vec[:, :])
        hin = P.din("hTin", [D, T])
        ropeT = P.din("ropeT", [4, 128, T])
        W = {n: P.din(n, list(sh)) for n, sh in A_W.items()}
        hT = P.dout("hT", [D, T])
        QA = P.dout("QA", [512, T], BF16)
        QB = P.dout("QB", [1152, T], BF16)
        QC = P.dout("QC", [768, T], BF16)
        KF = P.dout("KF", [KF_ROWS, T], BF16)
        VT = P.dout("VT", [T, D], BF16)
        _copy_h(P, hin, hT)
        if "noffn" not in DBG:
            phase_ffn(P, hT, VC_FFN1, W["ffn1_gate"], W["ffn1_up"], W["ffn1_down"])
        if "noa2" not in DBG:
            phase_a2(P, hT, ropeT, W, QA, QB, QC, KF, VT)
    elif name == "B":
        vec = P.din("vec", [128, NVC])
        P.load_vec(vec[:, :])
        hin = P.din("hTin", [D, T])
        QA = P.din("QA", [512, T], BF16)
        QB = P.din("QB", [1152, T], BF16)
        QC = P.din("QC", [768, T], BF16)
        KFall = P.din("KFall", [2, KF_ROWS, T], BF16)
        VTall = P.din("VTall", [2, T, D], BF16)
        memT = P.din("memT", [D, 256])
        cmask = P.din("cmask", [NQT, 8, 128, 512])
        dmask = P.din("dmask", [NQT, 24, 128, 512])
        W = {n: P.din(n, list(sh)) for n, sh in B_W.items()}
        hT = P.dout("hT", [D, T])
        _copy_h(P, hin, hT)
        phase_b(P, hT, QA, QB, QC, KFall, VTall, memT, cmask, dmask, W)
        phase_ffn(P, hT, VC_FFN2, W["ffn2_gate"], W["ffn2_up"], W["ffn2_down"])
    elif name == "Z":
        vec = P.din("vec", [128, NVC])
        P.load_vec(vec[:, :])
        hin = P.din("hTin", [D, T])
        hT = P.dint("hT", [D, T])
        _copy_h(P, hin, hT)
        out = P.dout("out", [T, D])
        phase_z(P, hT, out, VC_FINAL)
    P.k.barrier()
    _PROGS[name] = P
    return P


_CONSTS = None


def _run(name, in_maps):
    global _CONSTS
    P = _get_prog(name)
    if _CONSTS is None:
        _CONSTS = _consts()
    maps = [dict(m, **_CONSTS) for m in in_maps]
    res = run_bass_kernel_spmd(P.nc, maps, core_ids=list(range(8)))
    return res.results


def _masks(par):
    cm = np.zeros((NQT, 8, 128, 512), np.float32)
    dm = np.zeros((NQT, 24, 128, 512), np.float32)
    kk = np.arange(128)[:, None]
    qq = np.arange(512)[None, :]
    for j in range(NQT):
        g = GT[par][j]
        for slot in range(2):
            for kb in range(4):
                kpos = (2 * j + slot) * 512 + kb * 128 + kk
                cm[j, slot * 4 + kb] = (kpos <= g * 512 + qq)
        lo, hi = DIL_TILES[j]
        for kt in range(lo, hi):
            for kb in range(4):
                dlt = (g * 512 + qq) - (kt * 512 + kb * 128 + kk)
                w = ((dlt >= 0) & (dlt <= 128)).astype(np.float32)
                w += ((dlt >= 0) & (dlt <= 512) & (dlt % 4 == 0))
                w += ((dlt >= 0) & (dlt <= 2048) & (dlt % 16 == 0))
                dm[j, (kt - lo) * 4 + kb] = w
    return cm, dm


def _layer_weights(inp, l):
    c = np.ascontiguousarray
    w_in = inp["w_in"][l]
    A = dict(ffn1_gate=inp["ffn1_gate"][l], ffn1_up=inp["ffn1_up"][l], ffn1_down=inp["ffn1_down"][l])
    A["w_r64"] = c(w_in[:, 0:1024])
    A["w_r128"] = c(w_in[:, 2624:4160])
    A["w_kr"] = c(w_in[:, 2560:2624])
    A["w_c"] = c(w_in[:, 1536:2560])
    A["w_v"] = c(np.concatenate([w_in[:, 1024:1536], w_in[:, 4160:4928]], axis=1))
    uq = inp["mla_w_uq"][l].reshape(512, 6, 192)
    A["w_uq"] = c(np.concatenate([uq[:, :, 0:128].reshape(512, 768), uq[:, :, 128:192].reshape(512, 384)], axis=1))
    ukv = inp["mla_w_ukv"][l].reshape(512, 6, 256)
    A["w_ukv"] = c(np.concatenate([ukv[:, :, 0:128].reshape(512, 768), ukv[:, :, 128:256].reshape(512, 768)],
                                  axis=1))
    B = dict(w_out=inp["w_out"][l], cross_wq=inp["cross_wq"][l], cross_wkv=inp["cross_wkv"][l],
             cross_wo=inp["cross_wo"][l], ffn2_gate=inp["ffn2_gate"][l], ffn2_up=inp["ffn2_up"][l],
             ffn2_down=inp["ffn2_down"][l])
    return A, B


def kernel(**inp):
    inp = {k_: np.asarray(v) for k_, v in inp.items()}
    x, mem, pos = inp["x"], inp["mem"], inp["positions"]
    cores = [(c // 2, c % 2) for c in range(8)]
    maps = [dict(x=_core_tokens(x, b, par), pos=_core_tokens(pos[..., None], b, par)[:, 0].astype(np.int32),
                 mem=np.ascontiguousarray(mem[b])) for (b, par) in cores]
    r = _run("X", maps)
    hT = [r[c]["hT"] for c in range(8)]
    ropeT = [r[c]["ropeT"] for c in range(8)]
    memT = [r[c]["memT"] for c in range(8)]
    masks = [_masks(0), _masks(1)]
    for l in range(L):
        vec = _vec_pack(inp, l)
        A, B = _layer_weights(inp, l)
        r = _run("A", [dict(A, vec=vec, hTin=hT[c], ropeT=ropeT[c]) for c in range(8)])
        hT = [r[c]["hT"] for c in range(8)]
        bmaps = []
        for c, (b, par) in enumerate(cores):
            c0 = 2 * b
            bmaps.append(dict(B, vec=vec, hTin=hT[c], QA=r[c]["QA"], QB=r[c]["QB"], QC=r[c]["QC"],
                              KFall=np.stack([r[c0]["KF"], r[c0 + 1]["KF"]]),
                              VTall=np.stack([r[c0]["VT"], r[c0 + 1]["VT"]]),
                              memT=memT[c], cmask=masks[par][0], dmask=masks[par][1]))
        r = _run("B", bmaps)
        hT = [r[c]["hT"] for c in range(8)]
    vec = _vec_pack(inp, 0)
    r = _run("Z", [dict(vec=vec, hTin=hT[c]) for c in range(8)])
    out = np.zeros((4, 4096, D), np.float32)
    for c, (b, par) in enumerate(cores):
        for j, g in enumerate(GT[par]):
            out[b, g * 512:(g + 1) * 512] = r[c]["out"][j * 512:(j + 1) * 512]
    return out


def proj_fm(P, x, xb, W, kcn, ncols, epi, ntok=512, banks=(0, 1)):
    k, nc = P.k, P.nc
    cpt = min(ncols, (WSLOT // kcn) // 128 * 128) if ncols >= 128 else ncols
    tiles = []
    c0 = 0
    while c0 < ncols:
        w_ = min(cpt, ncols - c0)
        P.wpush(W[:, c0:c0 + w_], kcn, w_)
        tiles.append((c0, w_))
        c0 += w_
    it = 0
    for (c0, w_) in tiles:
        wt, wb = P.wnext()
        for s0 in range(0, w_, 128):
            m = min(128, w_ - s0)
            pa, pb = P.bank(banks[it % len(banks)])
            it += 1
            for c in range(kcn):
                k.op(k.pe, lambda pa=pa, c=c, s0=s0, m=m, wt=wt: nc.tensor.matmul(
                    pa[0:m, 0:ntok], wt[:, c, s0:s0 + m], x[:, c, 0:ntok], start=(c == 0), stop=(c == kcn - 1)),
                    reads=[wb, xb], writes=[pb], signal=(c == kcn - 1))
            epi((c0 + s0) // 128, pa, pb, m)


def proj_tm(P, x, xb, W, kcn, ncols, epi, ntb=4, banks=(2, 3)):
    k, nc = P.k, P.nc
    cpt = min(ncols, (WSLOT // kcn) // 128 * 128, 512)
    tiles = []
    c0 = 0
    while c0 < ncols:
        w_ = min(cpt, ncols - c0)
        P.wpush(W[:, c0:c0 + w_], kcn, w_)
        tiles.append((c0, w_))
        c0 += w_
    it = 0
    for (c0, w_) in tiles:
        wt, wb = P.wnext()
        for tb in range(ntb):
            pa, pb = P.bank(banks[it % len(banks)])
            it += 1
            for c in range(kcn):
                k.op(k.pe, lambda pa=pa, c=c, tb=tb, wt=wt, w_=w_: nc.tensor.matmul(
                    pa[:, 0:w_], x[:, c, tb * 128:(tb + 1) * 128], wt[:, c, 0:w_], start=(c == 0),
                    stop=(c == kcn - 1)), reads=[wb, xb], writes=[pb], signal=(c == kcn - 1))
            epi(tb, c0, w_, pa, pb)


class Scr:
    def __init__(self, P, st, name, shape, dtype, n):
        self.t = st.enter_context(P.nc.sbuf_tensor(name, [128, n] + list(shape), dtype))
        self.b = [P.k.buf(name) for _ in range(n)]
        self.n = n
        self.i = 0

    def next(self):
        s = self.i % self.n
        self.i += 1
        return self.t[:, s], self.b[s]


def phase_a2(P, hT, ropeT, W, QA, QB, QC, KF, VT):
    k, nc = P.k, P.nc
    with ExitStack() as st:
        hS = st.enter_context(nc.sbuf_tensor("a_h", [128, KC, 512], F32))
        hb = [k.buf("ah") for _ in range(KC)]
        xn = st.enter_context(nc.sbuf_tensor("a_xn", [128, KC, 512], BF16))
        xnb = k.buf("axn")
        rt = st.enter_context(nc.sbuf_tensor("a_rt", [128, 4, 512], F32))
        rtb = k.buf("art")
        cq = st.enter_context(nc.sbuf_tensor("a_cq", [128, 8, 512], F32))
        cqb = [k.buf("acq") for _ in range(8)]
        cn = st.enter_context(nc.sbuf_tensor("a_cn", [128, 8, 512], BF16))
        cnb = [k.buf("acn"), k.buf("acn")]
        R = rms_alloc(P, st, 512, "a")
        ub = Scr(P, st, "a_ub", [512], BF16, 2)
        ub2 = Scr(P, st, "a_ub2", [512], BF16, 2)
        ob = Scr(P, st, "a_ob", [512], BF16, 4)
        for tt in range(NQT):
            t0 = tt * 512
            for c in range(KC):
                k.dma(k.sp, hS[:, c, :], hT[c * 128:(c + 1) * 128, t0:t0 + 512], reads=[P.dbuf["hT"]],
                      writes=[hb[c]], tag=hb[c])
            k.dma(k.sp, rt[:], ropeT[:, :, t0:t0 + 512].rearrange("f p t -> p f t"), reads=[P.dbuf["ropeT"]],
                  writes=[rtb], tag=rtb)
            rstd, rb = rms_rstd(P, R, lambda c: (hS[:, c, :], hb[c]), KC, D, 1e-6, [6])
            for c in range(KC):
                k.op(k.dve, lambda c=c: nc.vector.scalar_tensor_tensor(
                    out=xn[:, c, :], in0=hS[:, c, :], scalar=P.vec[:, VC_MIX + c:VC_MIX + c + 1], in1=rstd[:],
                    op0=ALU.mult, op1=ALU.mult), reads=[hb[c], rb, P.vbuf], writes=[xnb])

            def store(dst, dbn, src_ap, src_b):
                k.dma(k.sp, dst, src_ap, reads=[src_b], writes=[P.dbuf[dbn]], tag=src_b)

            def rope_epi(pa, pb, m, kind, dst, dbn):
                if "norope" in DBG:
                    return plain_epi(pa, pb, m, dst, dbn)
                perm = P.perm64 if kind == 64 else P.perm128
                ci_, si_ = (0, 1) if kind == 64 else (2, 3)
                wa, wb_ = ub.next()
                va, vb_ = ub2.next()
                k.op(k.dve, lambda: nc.vector.scalar_tensor_tensor(
                    out=wa[0:m, :], in0=pa[0:m, :], scalar=1.0, in1=rt[0:m, ci_, :], op0=ALU.mult, op1=ALU.mult),
                    reads=[pb, rtb], writes=[wb_])
                k.op(k.dve, lambda: nc.vector.scalar_tensor_tensor(
                    out=va[0:m, :], in0=pa[0:m, :], scalar=-1.0, in1=rt[0:m, si_, :], op0=ALU.mult, op1=ALU.mult),
                    reads=[pb, rtb], writes=[vb_])
                sa, sb = P.bank(4 + (ub.i % 2))
                k.op(k.pe, lambda: nc.tensor.matmul(sa[0:m, :], P.identb[0:m, 0:m], wa[0:m, :], start=True,
                                                    stop=False), reads=[wb_, P.cbuf], writes=[sb], signal=False)
                k.op(k.pe, lambda: nc.tensor.matmul(sa[0:m, :], perm[0:m, 0:m], va[0:m, :], start=False,
                                                    stop=True), reads=[vb_, P.cbuf], writes=[sb])
                oa, obb = ob.next()
                k.op(k.act, lambda: nc.scalar.copy(out=oa[0:m, :], in_=sa[0:m, :]), reads=[sb], writes=[obb])
                store(dst, dbn, oa[0:m, :], obb)

            def plain_epi(pa, pb, m, dst, dbn):
                oa, obb = ob.next()
                k.op(k.act, lambda: nc.scalar.copy(out=oa[0:m, :], in_=pa[0:m, :]), reads=[pb], writes=[obb])
                store(dst, dbn, oa[0:m, :], obb)

            def e64(ci, pa, pb, m):
                if ci < 4:
                    rope_epi(pa, pb, m, 64, QA[ci * 128:(ci + 1) * 128, t0:t0 + 512], "QA")
                else:
                    r0 = KOFF_A + (ci - 4) * 128
                    rope_epi(pa, pb, m, 64, KF[r0:r0 + 128, t0:t0 + 512], "KF")
            proj_fm(P, xn, xnb, W["w_r64"], KC, 1024, e64)

            def e128(ci, pa, pb, m):
                if ci < 6:
                    rope_epi(pa, pb, m, 128, QC[ci * 128:(ci + 1) * 128, t0:t0 + 512], "QC")
                else:
                    r0 = KOFF_C + (ci - 6) * 128
                    rope_epi(pa, pb, m, 128, KF[r0:r0 + 128, t0:t0 + 512], "KF")
            proj_fm(P, xn, xnb, W["w_r128"], KC, 1536, e128)

            def ekr(ci, pa, pb, m):
                rope_epi(pa, pb, 64, 64, KF[KOFF_BR:KOFF_BR + 64, t0:t0 + 512], "KF")
            if "nokr" not in DBG:
                proj_fm(P, xn, xnb, W["w_kr"], KC, 64, ekr)

            def ec(ci, pa, pb, m):
                k.op(k.act, lambda: nc.scalar.copy(out=cq[:, ci, :], in_=pa), reads=[pb], writes=[cqb[ci]])
            proj_fm(P, xn, xnb, W["w_c"], KC, 1024, ec)

            def ev(tb, c0, w_, pa, pb):
                oa, obb = ob.next()
                k.op(k.act, lambda: nc.scalar.copy(out=oa[:, 0:w_], in_=pa[:, 0:w_]), reads=[pb], writes=[obb])
                for s0 in range(c0, c0 + w_, 128):
                    dcol = VOFF_A + s0 if s0 < 512 else VOFF_C + (s0 - 512)
                    store(VT[t0 + tb * 128:t0 + (tb + 1) * 128, dcol:dcol + 128], "VT",
                          oa[:, s0 - c0:s0 - c0 + 128], obb)
            if "nov" not in DBG:
                proj_tm(P, xn, xnb, W["w_v"], KC, 1280, ev)
            if "nomla" in DBG:
                continue

            for half, gcol in ((0, VC_QN), (1, VC_KVN)):
                rs, rsb = rms_rstd(P, R, lambda c, half=half: (cq[:, half * 4 + c, :], cqb[half * 4 + c]), 4, 512,
                                   1e-6, [6])
                for c in range(4):
                    k.op(k.dve, lambda c=c, half=half, gcol=gcol: nc.vector.scalar_tensor_tensor(
                        out=cn[:, half * 4 + c, :], in0=cq[:, half * 4 + c, :],
                        scalar=P.vec[:, gcol + c:gcol + c + 1], in1=rs[:], op0=ALU.mult, op1=ALU.mult),
                        reads=[cqb[half * 4 + c], rsb, P.vbuf], writes=[cnb[half]])

            def euq(ci, pa, pb, m):
                if ci < 6:
                    plain_epi(pa, pb, m, QB[ci * 128:(ci + 1) * 128, t0:t0 + 512], "QB")
                else:
                    rope_epi(pa, pb, m, 64, QB[ci * 128:(ci + 1) * 128, t0:t0 + 512], "QB")
            proj_fm(P, cn[:, 0:4], cnb[0], W["w_uq"], 4, 1152, euq)

            def euk(ci, pa, pb, m):
                r0 = KOFF_BN + ci * 128
                plain_epi(pa, pb, m, KF[r0:r0 + 128, t0:t0 + 512], "KF")
            proj_fm(P, cn[:, 4:8], cnb[1], W["w_ukv"][:, 0:768], 4, 768, euk)

            def euv(tb, c0, w_, pa, pb):
                oa, obb = ob.next()
                k.op(k.act, lambda: nc.scalar.copy(out=oa[:, 0:w_], in_=pa[:, 0:w_]), reads=[pb], writes=[obb])
                store(VT[t0 + tb * 128:t0 + (tb + 1) * 128, VOFF_B + c0:VOFF_B + c0 + w_], "VT", oa[:, 0:w_], obb)
            proj_tm(P, cn[:, 4:8], cnb[1], W["w_ukv"][:, 768:1536], 4, 768, euv)
        k.barrier()


def phase_mem(P, mem_in, memT):
    k, nc = P.k, P.nc
    with ExitStack() as st:
        mi = st.enter_context(nc.sbuf_tensor("m_in", [128, 2, D], F32))
        mib = [k.buf("mi"), k.buf("mi")]
        mf = st.enter_context(nc.sbuf_tensor("m_f", [128, KC, 256], F32))
        mfb = [k.buf("mf") for _ in range(KC)]
        R = rms_alloc(P, st, 256, "m")
        for tb in range(2):
            k.dma(k.sp, mi[:, tb, :], mem_in[tb * 128:(tb + 1) * 128, :], writes=[mib[tb]], tag=mib[tb])
        for c in range(KC):
            pa, pb = P.bank(c % 4)
            for tb in range(2):
                k.op(k.pe, lambda pa=pa, c=c, tb=tb: nc.tensor.transpose(
                    pa[:, tb * 128:(tb + 1) * 128], mi[:, tb, c * 128:(c + 1) * 128], P.ident[:]),
                    reads=[mib[tb], P.cbuf], writes=[pb], signal=(tb == 1))
            k.op(k.dve, lambda pa=pa, c=c: nc.vector.tensor_copy(out=mf[:, c, :], in_=pa[:, 0:256]),
                 reads=[pb], writes=[mfb[c]])
        rstd, rb = rms_rstd(P, R, lambda c: (mf[:, c, :], mfb[c]), KC, D, 1e-6, [6])
        for c in range(KC):
            k.op(k.dve, lambda c=c: nc.vector.tensor_tensor(out=mf[:, c, :], in0=mf[:, c, :], in1=rstd[:],
                                                            op=ALU.mult), reads=[mfb[c], rb], writes=[mfb[c]])
            k.dma(k.sp, memT[c * 128:(c + 1) * 128, :], mf[:, c, :], reads=[mfb[c]], writes=[P.dbuf["memT"]],
                  tag=mfb[c])
        k.barrier()


def phase_b(P, hT, QA, QB, QC, KFall, VTall, memT, cmask, dmask, W):
    k, nc = P.k, P.nc
    hTb = P.dbuf["hT"]
    with ExitStack() as st:
        kmT = st.enter_context(nc.sbuf_tensor("b_kmT", [128, 4, 256], BF16))
        kmb = k.buf("kmT")
        vm = st.enter_context(nc.sbuf_tensor("b_vm", [128, 2, 512], BF16))
        vmb = k.buf("vm")
        lamt = st.enter_context(nc.sbuf_tensor("b_lam", [128, 8], F32))
        lamb = k.buf("lam")
        lsc = st.enter_context(nc.sbuf_tensor("b_lsc", [128, 2, 64], F32))
        lscb = k.buf("lsc")
        for i in range(2):
            k.op(k.dve, lambda i=i: nc.vector.tensor_tensor(
                out=lsc[:, i, :], in0=P.vec[:, VC_LAM + 128 * i:VC_LAM + 128 * i + 64],
                in1=P.vec[:, VC_LAM + 128 * i + 64:VC_LAM + 128 * i + 128], op=ALU.mult),
                reads=[P.vbuf, P.cbuf], writes=[lscb])
            k.op(k.dve, lambda i=i: nc.vector.reduce_sum(out=lamt[:, i:i + 1], in_=lsc[:, i, :],
                                                        axis=mybir.AxisListType.X), reads=[lscb], writes=[lamb])
        k.op(k.act, lambda: nc.scalar.activation(out=lamt[:, 2:4], in_=lamt[:, 0:2], func=AF.Exp),
             reads=[lamb], writes=[lamb])
        k.op(k.dve, lambda: nc.vector.tensor_tensor(out=lamt[:, 4:5], in0=lamt[:, 3:4], in1=lamt[:, 2:3],
                                                    op=ALU.subtract), reads=[lamb], writes=[lamb])
        k.op(k.dve, lambda: nc.vector.tensor_tensor(out=lamt[:, 5:6], in0=lamt[:, 4:5],
                                                    in1=P.vec[:, VC_LCST:VC_LCST + 1], op=ALU.add),
             reads=[lamb, P.vbuf], writes=[lamb])
        k.op(k.dve, lambda: nc.vector.tensor_tensor(out=lamt[:, 6:7], in0=P.vec[:, VC_SUBLN:VC_SUBLN + 1],
                                                    in1=P.vec[:, VC_LCST + 1:VC_LCST + 2], op=ALU.mult),
             reads=[lamb, P.vbuf], writes=[lamb])
        neglam = lamt[:, 5:6]
        subcol = lamt[:, 6:7]
        with ExitStack() as st2:
            mf = st2.enter_context(nc.sbuf_tensor("b_mf", [128, KC, 256], F32))
            mfb = k.buf("bmf")
            mn = st2.enter_context(nc.sbuf_tensor("b_mn", [128, KC, 256], BF16))
            mnb = k.buf("bmn")
            k.dma(k.sp, mf[:], memT.rearrange("(c p) t -> p c t", p=128), reads=[P.dbuf["memT"]], writes=[mfb],
                  tag=mfb)
            for c in range(KC):
                k.op(k.dve, lambda c=c: nc.vector.tensor_scalar(
                    out=mn[:, c, :], in0=mf[:, c, :], scalar1=P.vec[:, VC_MEM + c:VC_MEM + c + 1], scalar2=None,
                    op0=ALU.mult), reads=[mfb, P.vbuf], writes=[mnb])

            def ekm(ci, pa, pb, m):
                k.op(k.act, lambda: nc.scalar.copy(out=kmT[:, ci, :], in_=pa[:, 0:256]), reads=[pb], writes=[kmb])
            proj_fm(P, mn, mnb, W["cross_wkv"][:, 0:512], KC, 512, ekm, ntok=256)

            def evm(tb, c0, w_, pa, pb):
                k.op(k.act, lambda: nc.scalar.copy(out=vm[:, tb, c0:c0 + w_], in_=pa[:, 0:w_]), reads=[pb],
                     writes=[vmb])
            proj_tm(P, mn, mnb, W["cross_wkv"][:, 512:1024], KC, 512, evm, ntb=2)
            k.barrier()
        yT = st.enter_context(nc.sbuf_tensor("b_yT", [128, KC, 512], BF16))
        yb = [k.buf("yT") for _ in range(KC)]
        xn = st.enter_context(nc.sbuf_tensor("b_xn", [128, KC, 512], BF16))
        xnb = k.buf("bxn")
        qc = st.enter_context(nc.sbuf_tensor("b_qc", [128, 4, 512], BF16))
        qcb = [k.buf("qc") for _ in range(4)]
        oc = st.enter_context(nc.sbuf_tensor("b_oc", [128, 4, 512], BF16))
        ocb = k.buf("oc")
        cm = st.enter_context(nc.sbuf_tensor("b_cm", [128, 8, 512], BF16))
        cmb = k.buf("cm")
        dm = st.enter_context(nc.sbuf_tensor("b_dm", [128, 24, 512], BF16))
        dmb = k.buf("dm")
        qs = Scr(P, st, "b_q", [2, 512], BF16, 2)
        kts = Scr(P, st, "b_k", [2, 512], BF16, 3)
        vts = Scr(P, st, "b_v", [4, 128], BF16, 3)
        pts = Scr(P, st, "b_p", [512], BF16, 4)
        f1 = Scr(P, st, "b_f1", [512], F32, 3)
        f2 = Scr(P, st, "b_f2", [512], F32, 3)
        hs = Scr(P, st, "b_hs", [512], F32, 3)
        os_ = Scr(P, st, "b_os", [512], F32, 3)
        sqs = Scr(P, st, "b_sq", [512], BF16, 2)
        R = rms_alloc(P, st, 512, "br")
        cnt = {"s": 0, "ol": 0}

        def attn(qparts, kload, vload, nblk, maskfn, scale):
            oa, ob_ = P.bank(2 + 2 * (cnt["ol"] % 2))
            la, lb_ = P.bank(3 + 2 * (cnt["ol"] % 2))
            cnt["ol"] += 1
            for i in range(nblk):
                kaps, kbufs = kload(i)
                vap, vbuf = vload(i)
                sa, sb = P.bank(cnt["s"] % 2)
                cnt["s"] += 1
                for pi, (qa, qb_, m) in enumerate(qparts):
                    k.op(k.pe, lambda sa=sa, pi=pi, qa=qa, m=m, kaps=kaps: nc.tensor.matmul(
                        sa, kaps[pi], qa, start=(pi == 0), stop=(pi == len(qparts) - 1)),
                        reads=kbufs + [qb_], writes=[sb], signal=(pi == len(qparts) - 1))
                pa_, pb_ = pts.next()
                k.op(k.act, lambda sa=sa, pa_=pa_: nc.scalar.activation(out=pa_, in_=sa, func=AF.Exp, scale=scale),
                     reads=[sb], writes=[pb_])
                mk = maskfn(i)
                if mk is not None:
                    k.op(k.dve, lambda pa_=pa_, mk=mk: nc.vector.tensor_tensor(out=pa_, in0=pa_, in1=mk[0],
                                                                              op=ALU.mult),
                         reads=[pb_, mk[1], cmb, dmb], writes=[pb_])
                k.op(k.pe, lambda oa=oa, vap=vap, pa_=pa_, i=i: nc.tensor.matmul(
                    oa, vap, pa_, start=(i == 0), stop=(i == nblk - 1)), reads=[vbuf, pb_], writes=[ob_],
                    signal=(i == nblk - 1))
                k.op(k.pe, lambda la=la, pa_=pa_, i=i: nc.tensor.matmul(
                    la, P.ones[:], pa_, start=(i == 0), stop=(i == nblk - 1)), reads=[P.cbuf, pb_], writes=[lb_])
            return (oa, ob_), (la, lb_)

        def norm_out(O, Lx, dst, dstb):
            ra, rb_ = f1.next()
            k.op(k.dve, lambda: nc.vector.reciprocal(out=ra, in_=Lx[0]), reads=[Lx[1]], writes=[rb_])
            k.op(k.dve, lambda: nc.vector.scalar_tensor_tensor(out=dst, in0=O[0], scalar=1.0, in1=ra,
                                                               op0=ALU.mult, op1=ALU.mult),
                 reads=[O[1], rb_], writes=[dstb])

        def dram_kv(j, ktiles, kspec, vcol):
            state = {}

            def ensure(i):
                kt = ktiles[i // 4]
                if state.get("kt") != (i // 4):
                    r, lj = OWNER[kt]
                    ka, kb_ = kts.next()
                    for pi, (row0, m) in enumerate(kspec):
                        k.dma(k.sp, ka[0:m, pi, :], KFall[r, row0:row0 + m, lj * 512:(lj + 1) * 512],
                              reads=[P.dbuf["KFall"]], writes=[kb_], tag=kb_)
                    va, vb_ = vts.next()
                    k.dma(k.sp, va, VTall[r, lj * 512:(lj + 1) * 512, vcol:vcol + 128].rearrange(
                        "(b p) c -> p b c", p=128), reads=[P.dbuf["VTall"]], writes=[vb_], tag=vb_)
                    state.update(kt=i // 4, ka=ka, kb=kb_, va=va, vb=vb_)

            def kload(i):
                ensure(i)
                b = i % 4
                return [state["ka"][0:m, pi, b * 128:(b + 1) * 128] for pi, (row0, m) in enumerate(kspec)], \
                    [state["kb"]]

            def vload(i):
                ensure(i)
                return state["va"][:, i % 4, :], state["vb"]
            return kload, vload

        for j in range(NQT):
            t0 = j * 512
            k.dma(k.pool, cm[:], cmask[j].rearrange("u p q -> p u q"), writes=[cmb], tag=cmb)
            lo, hi = DIL_TILES[j]
            nd = (hi - lo) * 4
            k.dma(k.pool, dm[:, 0:nd, :], dmask[j, 0:nd].rearrange("u p q -> p u q"), writes=[dmb], tag=dmb)
            ctiles = list(range(0, 2 * j + 2))

            def cmaskfn(i, j=j):
                kt = i // 4
                if kt < 2 * j:
                    return None
                return cm[:, (kt - 2 * j) * 4 + i % 4, :], cmb

            def dmaskfn(i):
                return dm[:, i, :], dmb

            def qload(parts):
                qa, qb_ = qs.next()
                out = []
                for pi, (src, dbn, m) in enumerate(parts):
                    k.dma(k.sp, qa[0:m, pi, :], src, reads=[P.dbuf[dbn]], writes=[qb_], tag=qb_)
                    out.append((qa[0:m, pi, :], qb_, m))
                return out

            for h in range(4):
                res = []
                for br in range(2):
                    qrow = br * 256 + h * 64
                    qp = qload([(QA[qrow:qrow + 64, t0:t0 + 512], "QA", 64)])
                    kl, vl = dram_kv(j, ctiles, [(KOFF_A + br * 256 + h * 64, 64)], VOFF_A + h * 128)
                    res.append(attn(qp, kl, vl, len(ctiles) * 4, cmaskfn, 0.125))
                a1, a1b = f1.next()
                norm_out(res[0][0], res[0][1], a1, a1b)
                a2, a2b = f2.next()
                norm_out(res[1][0], res[1][1], a2, a2b)
                o_, ob2 = f2.next()
                k.op(k.dve, lambda: nc.vector.scalar_tensor_tensor(out=o_, in0=a2, scalar=neglam, in1=a1,
                                                                   op0=ALU.mult, op1=ALU.add),
                     reads=[a1b, a2b, lamb], writes=[ob2])
                sq, sqb_ = sqs.next()
                k.op(k.act, lambda: nc.scalar.activation(out=sq, in_=o_, func=AF.Square), reads=[ob2], writes=[sqb_])
                pa, pb = P.bank(6)
                k.op(k.pe, lambda: nc.tensor.matmul(pa, P.ones[:], sq, start=True, stop=True),
                     reads=[sqb_, P.cbuf], writes=[pb])
                rs, rsb = f1.next()
                k.op(k.act, lambda: nc.scalar.activation(out=rs, in_=pa, func=AF.Sqrt, bias=P.cst[:, 6:7],
                                                         scale=1.0 / 128), reads=[pb, P.cbuf], writes=[rsb])
                k.op(k.dve, lambda: nc.vector.reciprocal(out=rs, in_=rs), reads=[rsb], writes=[rsb])
                k.op(k.dve, lambda: nc.vector.scalar_tensor_tensor(out=yT[:, h, :], in0=o_, scalar=subcol, in1=rs,
                                                                   op0=ALU.mult, op1=ALU.mult),
                     reads=[ob2, rsb, lamb], writes=[yb[h]])
            for h in range(6):
                qp = qload([(QB[h * 128:(h + 1) * 128, t0:t0 + 512], "QB", 128),
                            (QB[768 + h * 64:768 + (h + 1) * 64, t0:t0 + 512], "QB", 64)])
                kl, vl = dram_kv(j, ctiles, [(KOFF_BN + h * 128, 128), (KOFF_BR, 64)], VOFF_B + h * 128)
                O, Lx = attn(qp, kl, vl, len(ctiles) * 4, cmaskfn, 192 ** -0.5)
                norm_out(O, Lx, yT[:, 4 + h, :], yb[4 + h])
            dtiles = list(range(lo, hi))
            for h in range(6):
                qp = qload([(QC[h * 128:(h + 1) * 128, t0:t0 + 512], "QC", 128)])
                kl, vl = dram_kv(j, dtiles, [(KOFF_C + h * 128, 128)], VOFF_C + h * 128)
                O, Lx = attn(qp, kl, vl, len(dtiles) * 4, dmaskfn, 128 ** -0.5)
                norm_out(O, Lx, yT[:, 10 + h, :], yb[10 + h])

            def res_epi(ci, pa, pb, m):
                ha, hb_ = hs.next()
                k.dma(k.sp, ha, hT[ci * 128:(ci + 1) * 128, t0:t0 + 512], reads=[hTb], writes=[hb_], tag=hb_)
                oa_, ob_ = os_.next()
                k.op(k.dve, lambda: nc.vector.scalar_tensor_tensor(out=oa_, in0=pa, scalar=1.0, in1=ha,
                                                                   op0=ALU.mult, op1=ALU.add),
                     reads=[pb, hb_], writes=[ob_])
                k.dma(k.sp, hT[ci * 128:(ci + 1) * 128, t0:t0 + 512], oa_, reads=[ob_], writes=[hTb], tag=ob_)
            _proj_multi(P, yT, yb, W["w_out"], KC, D, res_epi)

            def hload(c):
                ha, hb_ = hs.next()
                k.dma(k.sp, ha, hT[c * 128:(c + 1) * 128, t0:t0 + 512], reads=[hTb], writes=[hb_], tag=hb_)
                return ha, hb_
            rstd, rb = rms_rstd(P, R, hload, KC, D, 1e-6, [6])
            for c in range(KC):
                ha, hb_ = hload(c)
                k.op(k.dve, lambda c=c, ha=ha: nc.vector.scalar_tensor_tensor(
                    out=xn[:, c, :], in0=ha, scalar=P.vec[:, VC_CROSS + c:VC_CROSS + c + 1], in1=rstd[:],
                    op0=ALU.mult, op1=ALU.mult), reads=[hb_, rb, P.vbuf], writes=[xnb])

            def eq(ci, pa, pb, m):
                k.op(k.act, lambda: nc.scalar.copy(out=qc[:, ci, :], in_=pa), reads=[pb], writes=[qcb[ci]])
            proj_fm(P, xn, xnb, W["cross_wq"], KC, 512, eq, banks=(6, 7))
            for h in range(4):
                qp = [(qc[:, h, :], qcb[h], 128)]
                O, Lx = attn(qp, lambda i, h=h: ([kmT[:, h, i * 128:(i + 1) * 128]], [kmb]),
                             lambda i, h=h: (vm[:, i, h * 128:(h + 1) * 128], vmb), 2, lambda i: None, 128 ** -0.5)
                norm_out(O, Lx, oc[:, h, :], ocb)
            proj_fm(P, oc, ocb, W["cross_wo"], 4, D, res_epi, banks=(6, 7))
        k.barrier()


def _proj_multi(P, x, xbs, W, kcn, ncols, epi, banks=(6, 7)):
    k, nc = P.k, P.nc
    cpt = (WSLOT // kcn) // 128 * 128
    tiles = []
    for c0 in range(0, ncols, cpt):
        P.wpush(W[:, c0:c0 + cpt], kcn, cpt)
        tiles.append(c0)
    it = 0
    for c0 in tiles:
        wt, wb = P.wnext()
        for s0 in range(0, cpt, 128):
            pa, pb = P.bank(banks[it % len(banks)])
            it += 1
            for c in range(kcn):
                k.op(k.pe, lambda pa=pa, c=c, s0=s0, wt=wt: nc.tensor.matmul(
                    pa, wt[:, c, s0:s0 + 128], x[:, c, :], start=(c == 0), stop=(c == kcn - 1)),
                    reads=[wb, xbs[c]], writes=[pb], signal=(c == kcn - 1))
            epi((c0 + s0) // 128, pa, pb, 128)
```

```python
import math
from contextlib import ExitStack
import numpy as np
import concourse.bass as bass
import concourse.mybir as mybir
from concourse.bass_utils import run_bass_kernel_spmd

F32, BF16, I32 = mybir.dt.float32, mybir.dt.bfloat16, mybir.dt.int32
AF = mybir.ActivationFunctionType
ALU = mybir.AluOpType

D = 2048
FF = 5632
L = 4
T = 2048
KC = D // 128
FC = FF // 128
NQT = 4
GT = ((0, 3, 4, 7), (1, 2, 5, 6))
OWNER = {}
for _p in (0, 1):
    for _j, _g in enumerate(GT[_p]):
        OWNER[_g] = (_p, _j)
KF_ROWS = 2112
KOFF_A, KOFF_BN, KOFF_BR, KOFF_C = 0, 512, 1280, 1344
VOFF_A, VOFF_B, VOFF_C = 0, 512, 1280
DIL_TILES = ((0, 2), (0, 4), (0, 6), (2, 8))
WSLOT = 4096
NWS = 4
FFN_TT = 1024
VC_FFN1, VC_MIX, VC_CROSS, VC_FFN2, VC_MEM = 0, 16, 32, 48, 64
VC_QN, VC_KVN, VC_SUBLN, VC_LAM, VC_LCST, VC_FINAL = 80, 84, 88, 89, 345, 347
NVC = 363


TRACE = None


def check_trace(tr):
    pos = {e: 0 for e in tr}
    sems = {}
    prog = True
    while prog:
        prog = False
        for e, lst in tr.items():
            while pos[e] < len(lst):
                kind, sem, v = lst[pos[e]]
                if kind == "w":
                    if sems.get(sem, 0) < v:
                        break
                else:
                    sems[sem] = sems.get(sem, 0) + v
                pos[e] += 1
                prog = True
    stuck = {e: (pos[e], len(lst), lst[pos[e]], sems.get(lst[pos[e]][1], 0)) for e, lst in tr.items()
             if pos[e] < len(lst)}
    return stuck


class Buf:
    __slots__ = ("name", "w", "r", "dsem", "dcnt")

    def __init__(self, name):
        self.name = name
        self.w = []
        self.r = []
        self.dsem = None
        self.dcnt = 0


class Eng:
    def __init__(self, name, obj, sem):
        self.name, self.obj, self.sem = name, obj, sem
        self.cnt = 0
        self.seen = {}


def _merge(lst, ev):
    out = [e for e in lst if e[0] is not ev[0]]
    out.append(ev)
    return out


class KB:
    def __init__(self, nc):
        self.nc = nc
        self.pe = Eng("pe", nc.tensor, nc.alloc_semaphore("s_pe"))
        self.act = Eng("act", nc.scalar, nc.alloc_semaphore("s_act"))
        self.dve = Eng("dve", nc.vector, nc.alloc_semaphore("s_dve"))
        self.pool = Eng("pool", nc.gpsimd, nc.alloc_semaphore("s_pool"))
        self.sp = Eng("sp", nc.sync, nc.alloc_semaphore("s_sp"))
        self.engs = [self.pe, self.act, self.dve, self.pool, self.sp]
        self.dsems = []
        self.nbuf = 0

    def buf(self, name="b"):
        self.nbuf += 1
        return Buf(f"{name}{self.nbuf}")

    def _wait(self, E, evs):
        for (sem, val) in evs:
            if E is self.pe and sem is self.pe.sem:
                continue
            if E.seen.get(sem, 0) < val:
                E.obj.wait_ge(sem, val)
                E.seen[sem] = val
                if TRACE is not None:
                    TRACE.setdefault(E.name, []).append(("w", id(sem), val))

    def op(self, E, fn, reads=(), writes=(), signal=True):
        evs = []
        for b in reads:
            evs += b.w
        for b in writes:
            evs += b.w
            evs += b.r
        self._wait(E, evs)
        ins = fn()
        if signal:
            E.cnt += 1
            ins.then_inc(E.sem, 1)
            ev = (E.sem, E.cnt)
            if TRACE is not None:
                TRACE.setdefault(E.name, []).append(("i", id(E.sem), 1))
        else:
            ev = (E.sem, E.cnt + 1)
        for b in reads:
            b.r = _merge(b.r, ev)
        for b in writes:
            b.w = _merge(b.w, ev)
            b.r = []
        return ins

    def dma(self, Q, out, in_, reads=(), writes=(), tag=None):
        if tag.dsem is None:
            tag.dsem = self.nc.alloc_semaphore("d_" + tag.name)
            self.dsems.append(tag)
        evs = []
        if tag.dcnt:
            evs.append((tag.dsem, tag.dcnt))
        for b in reads:
            evs += b.w
        for b in writes:
            evs += b.w
            evs += b.r
        self._wait(Q, evs)
        tag.dcnt += 16
        Q.obj.dma_start(out=out, in_=in_).then_inc(tag.dsem, 16)
        if TRACE is not None:
            TRACE.setdefault(Q.name, []).append(("i", id(tag.dsem), 16))
        ev = (tag.dsem, tag.dcnt)
        for b in reads:
            b.r = _merge(b.r, ev)
        for b in writes:
            b.w = _merge(b.w, ev)
            b.r = []

    def barrier(self, engines=None):
        evs = [(e.sem, e.cnt) for e in self.engs if e.cnt]
        evs += [(b.dsem, b.dcnt) for b in self.dsems if b.dcnt]
        for E in (engines or self.engs):
            self._wait(E, evs)


class Prog:
    def __init__(self, mode):
        self.mode = mode
        self.nc = bass.Bass("TRN2", target_bir_lowering=False)
        self.k = KB(self.nc)
        self.st = ExitStack()
        self.dram = {}
        self.dbuf = {}
        nc = self.nc
        self.ps = self.st.enter_context(nc.psum_tensor("ps", [128, 8, 512], F32))
        self.psb = [self.k.buf("psum") for _ in range(8)]
        self.wring = self.st.enter_context(nc.sbuf_tensor("sb_wring", [128, NWS, WSLOT], BF16))
        self.wbuf = [self.k.buf("w") for _ in range(NWS)]
        self.wq = []
        self.wissued = 0
        self.wcons = 0
        self.cst = self.st.enter_context(nc.sbuf_tensor("sb_cst", [128, 8], F32))
        self.ident = self.st.enter_context(nc.sbuf_tensor("sb_ident", [128, 128], F32))
        self.perm64 = self.st.enter_context(nc.sbuf_tensor("sb_perm64", [128, 128], BF16))
        self.perm128 = self.st.enter_context(nc.sbuf_tensor("sb_perm128", [128, 128], BF16))
        self.ones = self.st.enter_context(nc.sbuf_tensor("sb_ones", [128, 128], BF16))
        self.identb = self.st.enter_context(nc.sbuf_tensor("sb_identb", [128, 128], BF16))
        self.vec = self.st.enter_context(nc.sbuf_tensor("sb_vec", [128, NVC], F32))
        self.cbuf = self.k.buf("const")
        self.vbuf = self.k.buf("vec")

    def din(self, name, shape, dtype=F32):
        t = self.nc.dram_tensor(name, list(shape), dtype, kind="ExternalInput").ap()
        self.dram[name] = t
        self.dbuf[name] = self.k.buf(name)
        return t

    def dout(self, name, shape, dtype=F32):
        t = self.nc.dram_tensor(name, list(shape), dtype, kind="ExternalOutput").ap()
        self.dram[name] = t
        self.dbuf[name] = self.k.buf(name)
        return t

    def dint(self, name, shape, dtype=F32):
        t = self.nc.dram_tensor(name, list(shape), dtype, kind="Internal").ap()
        self.dram[name] = t
        self.dbuf[name] = self.k.buf(name)
        return t

    def wpush(self, src, kcn, ncols):
        assert kcn * ncols <= WSLOT
        self.wq.append((src, kcn, ncols))

    def _wissue(self):
        i = self.wissued
        src, kcn, ncols = self.wq[i]
        s = i % NWS
        dst = self.wring[:, s, 0:kcn * ncols].rearrange("p (c n) -> p c n", n=ncols)
        self.k.dma(self.k.pool, dst, src.rearrange("(c p) n -> p c n", p=128),
                   writes=[self.wbuf[s]], tag=self.wbuf[s])
        self.wissued += 1

    def wnext(self):
        i = self.wcons
        while self.wissued < len(self.wq) and self.wissued < i + NWS - 1:
            self._wissue()
        assert self.wissued > i
        src, kcn, ncols = self.wq[i]
        s = i % NWS
        self.wcons += 1
        view = self.wring[:, s, 0:kcn * ncols].rearrange("p (c n) -> p c n", n=ncols)
        return view, self.wbuf[s]

    def load_consts(self):
        k, nc = self.k, self.nc
        cst = self.din("cst", [128, 8])
        ident = self.din("ident", [128, 128])
        p64 = self.din("perm64", [128, 128])
        p128 = self.din("perm128", [128, 128])
        k.dma(k.sp, self.cst[:], cst[:, :], writes=[self.cbuf], tag=self.cbuf)
        k.dma(k.sp, self.ident[:], ident[:, :], writes=[self.cbuf], tag=self.cbuf)
        ptag = k.buf("constp")
        k.dma(k.pool, self.perm64[:], p64[:, :], writes=[self.cbuf], tag=ptag)
        k.dma(k.pool, self.perm128[:], p128[:, :], writes=[self.cbuf], tag=ptag)
        k.dma(k.pool, self.identb[:], ident[:, :], writes=[self.cbuf], tag=ptag)
        k.op(k.dve, lambda: nc.vector.memset(self.ones[:], 1.0), writes=[self.cbuf])

    def load_vec(self, src):
        k = self.k
        k.dma(k.sp, self.vec[:], src, writes=[self.vbuf], tag=self.vbuf)

    def bank(self, i):
        return self.ps[:, i, :], self.psb[i]


def rms_alloc(P, st, ncol, name):
    k, nc = P.k, P.nc
    R = {}
    R["sq"] = st.enter_context(nc.sbuf_tensor(name + "_sq", [128, 2, ncol], BF16))
    R["sqb"] = [k.buf("sq"), k.buf("sq")]
    R["rstd"] = st.enter_context(nc.sbuf_tensor(name + "_rstd", [128, ncol], F32))
    R["rb"] = k.buf("rstd")
    R["ncol"] = ncol
    return R


def rms_rstd(P, R, chunks, nch, dim, eps, bank_ids):
    k, nc = P.k, P.nc
    sq, sqb, rstd, rb, ncol = R["sq"], R["sqb"], R["rstd"], R["rb"], R["ncol"]
    nh = max(1, ncol // 512)
    w = min(ncol, 512)
    for c in range(nch):
        ap, b = chunks(c)
        s = c % 2
        k.op(k.act, lambda ap=ap, s=s: nc.scalar.activation(out=sq[:, s, :], in_=ap, func=AF.Square),
             reads=[b], writes=[sqb[s]])
        for h in range(nh):
            pa, pb = P.bank(bank_ids[h])
            k.op(k.pe, lambda pa=pa, s=s, h=h, c=c: nc.tensor.matmul(
                pa[:, 0:w], P.ones[:], sq[:, s, h * w:(h + 1) * w], start=(c == 0), stop=(c == nch - 1)),
                reads=[sqb[s], P.cbuf], writes=[pb])
    ecol = 5 if eps == 1e-6 else 6
    for h in range(nh):
        pa, pb = P.bank(bank_ids[h])
        k.op(k.act, lambda pa=pa, h=h: nc.scalar.activation(
            out=rstd[:, h * w:(h + 1) * w], in_=pa[:, 0:w], func=AF.Sqrt, bias=P.cst[:, ecol:ecol + 1],
            scale=1.0 / dim), reads=[pb, P.cbuf], writes=[rb])
    k.op(k.dve, lambda: nc.vector.reciprocal(out=rstd[:], in_=rstd[:]), reads=[rb], writes=[rb])
    return rstd, rb


def phase_x(P, x_in, pos_in, hT, ropeT):
    k, nc = P.k, P.nc
    with ExitStack() as st:
        xin = st.enter_context(nc.sbuf_tensor("xin", [128, 2, D], F32))
        xb = [k.buf("xin"), k.buf("xin")]
        xo = st.enter_context(nc.sbuf_tensor("xo", [128, 4, 512], F32))
        xob = [k.buf("xo") for _ in range(4)]
        posi = st.enter_context(nc.sbuf_tensor("posi", [128, T], I32))
        posf = st.enter_context(nc.sbuf_tensor("posf", [128, T], F32))
        ang = st.enter_context(nc.sbuf_tensor("ang", [128, T], F32))
        tab = st.enter_context(nc.sbuf_tensor("tab", [128, T], F32))
        pb_, fb_, ab_, tb_ = k.buf("posi"), k.buf("posf"), k.buf("ang"), k.buf("tab")
        k.dma(k.sp, posi[:], pos_in.partition_broadcast(128), writes=[pb_], tag=pb_)
        k.op(k.dve, lambda: nc.vector.tensor_copy(out=posf[:], in_=posi[:]), reads=[pb_], writes=[fb_])
        for ti in range(4):
            inv_col = 0 if ti < 2 else 2
            sgn_col = 1 if ti < 2 else 3
            is_cos = (ti % 2 == 0)
            k.op(k.dve, lambda inv_col=inv_col: nc.vector.tensor_scalar(
                out=ang[:], in0=posf[:], scalar1=P.cst[:, inv_col:inv_col + 1], scalar2=None, op0=ALU.mult),
                reads=[fb_, P.cbuf], writes=[ab_])
            C1 = 6.28125
            C2 = 2.0 * math.pi - C1
            if is_cos:
                k.op(k.dve, lambda: nc.vector.tensor_scalar(
                    out=ang[:], in0=ang[:], scalar1=math.pi / 2, scalar2=None, op0=ALU.add),
                    reads=[ab_], writes=[ab_])
            k.op(k.dve, lambda: nc.vector.tensor_scalar(
                out=posi[:], in0=ang[:], scalar1=1.0 / (2 * math.pi), scalar2=None, op0=ALU.mult),
                reads=[ab_], writes=[pb_])
            k.op(k.dve, lambda: nc.vector.tensor_copy(out=tab[:], in_=posi[:]), reads=[pb_], writes=[tb_])
            k.op(k.dve, lambda: nc.vector.scalar_tensor_tensor(
                out=ang[:], in0=tab[:], scalar=-C1, in1=ang[:], op0=ALU.mult, op1=ALU.add),
                reads=[tb_, ab_], writes=[ab_])
            k.op(k.dve, lambda: nc.vector.scalar_tensor_tensor(
                out=ang[:], in0=tab[:], scalar=-C2, in1=ang[:], op0=ALU.mult, op1=ALU.add),
                reads=[tb_, ab_], writes=[ab_])
            k.op(k.dve, lambda: nc.vector.tensor_scalar(
                out=ang[:], in0=ang[:], scalar1=-3.1415925, scalar2=3.1415925, op0=ALU.max, op1=ALU.min),
                reads=[ab_], writes=[ab_])
            k.op(k.act, lambda: nc.scalar.activation(out=tab[:], in_=ang[:], func=AF.Sin),
                 reads=[ab_], writes=[tb_])
            if not is_cos:
                k.op(k.dve, lambda sgn_col=sgn_col: nc.vector.tensor_scalar(
                    out=tab[:], in0=tab[:], scalar1=P.cst[:, sgn_col:sgn_col + 1], scalar2=None, op0=ALU.mult),
                    reads=[tb_, P.cbuf], writes=[tb_])
            k.dma(k.sp, ropeT[ti, :, :], tab[:], reads=[tb_], writes=[P.dbuf["ropeT"]], tag=tb_)
        nb = T // 128
        for tb in range(nb):
            s = tb % 2
            k.dma(k.sp, xin[:, s, :], x_in[tb * 128:(tb + 1) * 128, :], writes=[xb[s]], tag=xb[s])
            for cg in range(4):
                bi = (tb * 4 + cg) % 8
                pa, pb = P.bank(bi)
                for ci in range(4):
                    c = cg * 4 + ci
                    k.op(k.pe, lambda pa=pa, ci=ci, c=c, s=s: nc.tensor.transpose(
                        pa[:, ci * 128:(ci + 1) * 128], xin[:, s, c * 128:(c + 1) * 128], P.ident[:]),
                        reads=[xb[s], P.cbuf], writes=[pb], signal=(ci == 3))
                o = (tb * 4 + cg) % 4
                k.op(k.dve, lambda pa=pa, o=o: nc.vector.tensor_copy(out=xo[:, o, :], in_=pa),
                     reads=[pb], writes=[xob[o]])
                dst = hT[cg * 512:(cg + 1) * 512, tb * 128:(tb + 1) * 128].rearrange("(c p) t -> p c t", p=128)
                k.dma(k.sp, dst, xo[:, o, :].rearrange("p (c t) -> p c t", t=128),
                      reads=[xob[o]], writes=[P.dbuf["hT"]], tag=xob[o])
        k.barrier()


def phase_z(P, hT, out, gcol):
    k, nc = P.k, P.nc
    with ExitStack() as st:
        hS = st.enter_context(nc.sbuf_tensor("z_h", [128, KC, 512], F32))
        hb = [k.buf("zh") for _ in range(KC)]
        yo = st.enter_context(nc.sbuf_tensor("z_o", [128, 2, D], F32))
        yb = [k.buf("zo"), k.buf("zo")]
        R = rms_alloc(P, st, 512, "z")
        for tt in range(T // 512):
            for c in range(KC):
                k.dma(k.sp, hS[:, c, :], hT[c * 128:(c + 1) * 128, tt * 512:(tt + 1) * 512],
                      reads=[P.dbuf["hT"]], writes=[hb[c]], tag=hb[c])
            if True:
                rstd, rb = rms_rstd(P, R, lambda c: (hS[:, c, :], hb[c]), KC, D, 1e-6, [0])
                for c in range(KC):
                    k.op(k.dve, lambda c=c: nc.vector.scalar_tensor_tensor(
                        out=hS[:, c, :], in0=hS[:, c, :], scalar=P.vec[:, gcol + c:gcol + c + 1], in1=rstd[:],
                        op0=ALU.mult, op1=ALU.mult), reads=[hb[c], rb, P.vbuf], writes=[hb[c]])
                for tb in range(4):
                    s = tb % 2
                    for cg in range(4):
                        bi = 4 + (tb * 4 + cg) % 4
                        pa, pb = P.bank(bi)
                        for ci in range(4):
                            c = cg * 4 + ci
                            k.op(k.pe, lambda pa=pa, ci=ci, c=c, tb=tb: nc.tensor.transpose(
                                pa[:, ci * 128:(ci + 1) * 128], hS[:, c, tb * 128:(tb + 1) * 128], P.ident[:]),
                                reads=[hb[c], P.cbuf], writes=[pb], signal=(ci == 3))
                        k.op(k.act, lambda pa=pa, s=s, cg=cg: nc.scalar.copy(
                            out=yo[:, s, cg * 512:(cg + 1) * 512], in_=pa), reads=[pb], writes=[yb[s]])
                    r0 = tt * 512 + tb * 128
                    k.dma(k.sp, out[r0:r0 + 128, :], yo[:, s, :], reads=[yb[s]], writes=[P.dbuf["out"]],
                          tag=yb[s])
        k.barrier()


def phase_ffn(P, hT, gcol, Wg, Wu, Wd, TT=512):
    k, nc = P.k, P.nc
    NH = TT // 512
    with ExitStack() as st:
        xn = st.enter_context(nc.sbuf_tensor("f_xn", [128, KC, TT], BF16))
        xnb = k.buf("xn")
        HT = st.enter_context(nc.sbuf_tensor("f_ht", [128, FC, TT], BF16))
        htb = [k.buf("ht") for _ in range(FC)]
        hbuf = st.enter_context(nc.sbuf_tensor("f_h", [128, 3, TT], F32))
        hbb = [k.buf("fh") for _ in range(3)]
        sg = st.enter_context(nc.sbuf_tensor("f_sg", [128, 2, 512], F32))
        sgb = [k.buf("sg"), k.buf("sg")]
        ob = st.enter_context(nc.sbuf_tensor("f_o", [128, 3, 512], F32))
        obb = [k.buf("fo") for _ in range(3)]
        hTb = P.dbuf["hT"]
        R = rms_alloc(P, st, TT, "f")
        for tt in range(T // TT):
            t0 = tt * TT
            for fg in range(FF // 256):
                P.wpush(Wg[:, fg * 256:(fg + 1) * 256], KC, 256)
                P.wpush(Wu[:, fg * 256:(fg + 1) * 256], KC, 256)
            fblocks = [(0, 16), (16, 16), (32, 12)]
            for dg in range(D // 256):
                for (f0, fn_) in fblocks:
                    P.wpush(Wd[f0 * 128:(f0 + fn_) * 128, dg * 256:(dg + 1) * 256], fn_, 256)
            hcnt = [0]

            def hload(c):
                s = hcnt[0] % 3
                hcnt[0] += 1
                k.dma(k.sp, hbuf[:, s, :], hT[c * 128:(c + 1) * 128, t0:t0 + TT], reads=[hTb],
                      writes=[hbb[s]], tag=hbb[s])
                return hbuf[:, s, :], hbb[s]

            if True:
                rstd, rb = rms_rstd(P, R, hload, KC, D, 1e-6, [6, 7])
                for c in range(KC):
                    ap, b = hload(c)
                    k.op(k.dve, lambda ap=ap, c=c: nc.vector.scalar_tensor_tensor(
                        out=xn[:, c, :], in0=ap, scalar=P.vec[:, gcol + c:gcol + c + 1], in1=rstd[:],
                        op0=ALU.mult, op1=ALU.mult), reads=[b, rb, P.vbuf], writes=[xnb])
            it = 0
            for fg in range(FF // 256):
                wg, wgb = P.wnext()
                wu, wub = P.wnext()
                for sub in range(2):
                    fc = fg * 2 + sub
                    for h in range(NH):
                        ga, gb = P.bank(it % 2)
                        ua, ub = P.bank(2 + it % 2)
                        for c in range(KC):
                            k.op(k.pe, lambda ga=ga, c=c, sub=sub, h=h, wg=wg: nc.tensor.matmul(
                                ga, wg[:, c, sub * 128:(sub + 1) * 128], xn[:, c, h * 512:(h + 1) * 512],
                                start=(c == 0), stop=(c == KC - 1)),
                                reads=[wgb, xnb], writes=[gb], signal=(c == KC - 1))
                        for c in range(KC):
                            k.op(k.pe, lambda ua=ua, c=c, sub=sub, h=h, wu=wu: nc.tensor.matmul(
                                ua, wu[:, c, sub * 128:(sub + 1) * 128], xn[:, c, h * 512:(h + 1) * 512],
                                start=(c == 0), stop=(c == KC - 1)),
                                reads=[wub, xnb], writes=[ub], signal=(c == KC - 1))
                        s = it % 2
                        k.op(k.act, lambda ga=ga, s=s: nc.scalar.activation(out=sg[:, s, :], in_=ga, func=AF.Silu),
                             reads=[gb], writes=[sgb[s]])
                        k.op(k.dve, lambda ua=ua, s=s, fc=fc, h=h: nc.vector.tensor_tensor(
                            out=HT[:, fc, h * 512:(h + 1) * 512], in0=ua, in1=sg[:, s, :], op=ALU.mult),
                            reads=[ub, sgb[s]], writes=[htb[fc]])
                        it += 1
            oc = 0
            for dg in range(D // 256):
                base = 4 if dg % 2 == 0 else 0
                wts = []
                for bi_, (f0, fn_) in enumerate(fblocks):
                    wd, wdb = P.wnext()
                    for j in range(fn_):
                        fc = f0 + j
                        for sub in range(2):
                            for h in range(NH):
                                pa, pb = P.bank(base + sub * NH + h)
                                k.op(k.pe, lambda pa=pa, wd=wd, j=j, sub=sub, fc=fc, h=h: nc.tensor.matmul(
                                    pa, wd[:, j, sub * 128:(sub + 1) * 128], HT[:, fc, h * 512:(h + 1) * 512],
                                    start=(fc == 0), stop=(fc == FC - 1)),
                                    reads=[wdb, htb[fc]], writes=[pb], signal=(fc == FC - 1))
                for sub in range(2):
                    dc = dg * 2 + sub
                    ap, b = hload(dc)
                    for h in range(NH):
                        pa, pb = P.bank(base + sub * NH + h)
                        s = oc % 3
                        oc += 1
                        k.op(k.dve, lambda pa=pa, ap=ap, s=s, h=h: nc.vector.scalar_tensor_tensor(
                            out=ob[:, s, :], in0=pa, scalar=0.5, in1=ap[:, h * 512:(h + 1) * 512],
                            op0=ALU.mult, op1=ALU.add), reads=[pb, b], writes=[obb[s]])
                        k.dma(k.sp, hT[dc * 128:(dc + 1) * 128, t0 + h * 512:t0 + (h + 1) * 512], ob[:, s, :],
                              reads=[obb[s]], writes=[hTb], tag=obb[s])
            k.barrier()


def _consts():
    p = np.arange(128)
    cst = np.zeros((128, 8), np.float32)
    inv64 = 1.0 / (10000.0 ** (np.arange(0, 64, 2, dtype=np.float32) / 64))
    inv128 = 1.0 / (10000.0 ** (np.arange(0, 128, 2, dtype=np.float32) / 128))
    cst[:, 0] = inv64[p % 32]
    cst[:, 1] = np.where((p % 64) < 32, -1.0, 1.0)
    cst[:, 2] = inv128[p % 64]
    cst[:, 3] = np.where(p < 64, -1.0, 1.0)
    cst[:, 5] = 1e-6
    cst[:, 6] = 1e-5
    ident = np.eye(128, dtype=np.float32)
    perm64 = np.zeros((128, 128), np.float32)
    perm128 = np.zeros((128, 128), np.float32)
    for m in range(128):
        perm64[(m // 64) * 64 + ((m % 64) + 32) % 64, m] = 1.0
        perm128[(m + 64) % 128, m] = 1.0
    return dict(cst=cst, ident=ident, perm64=perm64, perm128=perm128)


def _vec_pack(inp, l):
    v = np.zeros((128, NVC), np.float32)

    def col16(a):
        return np.ascontiguousarray(a.reshape(-1, 128).T)
    v[:, VC_FFN1:VC_FFN1 + 16] = col16(inp["ffn1_norm"][l])
    v[:, VC_MIX:VC_MIX + 16] = col16(inp["mix_norm"][l])
    v[:, VC_CROSS:VC_CROSS + 16] = col16(inp["cross_norm"][l])
    v[:, VC_FFN2:VC_FFN2 + 16] = col16(inp["ffn2_norm"][l])
    v[:, VC_MEM:VC_MEM + 16] = col16(inp["mem_norm"][l])
    v[:, VC_QN:VC_QN + 4] = col16(inp["mla_q_norm"][l])
    v[:, VC_KVN:VC_KVN + 4] = col16(inp["mla_kv_norm"][l])
    v[:, VC_SUBLN] = inp["diff_subln"][l]
    v[:, VC_LAM:VC_LAM + 256] = inp["diff_lambda"][l].reshape(1, 256)
    lam_init = 0.8 - 0.6 * math.exp(-0.3 * l)
    v[:, VC_LCST] = -lam_init
    v[:, VC_LCST + 1] = 1.0 - lam_init
    v[:, VC_FINAL:VC_FINAL + 16] = col16(inp["final_norm"])
    return v


def _core_tokens(a, b, par):
    return np.ascontiguousarray(np.concatenate([a[b, g * 512:(g + 1) * 512] for g in GT[par]], axis=0))


_PROGS = {}
import os
DBG = os.environ.get('KDBG', '')
A_W = dict(ffn1_gate=(D, FF), ffn1_up=(D, FF), ffn1_down=(FF, D), w_r64=(D, 1024), w_r128=(D, 1536), w_kr=(D, 64),
           w_c=(D, 1024), w_v=(D, 1280), w_uq=(512, 1152), w_ukv=(512, 1536))
B_W = dict(w_out=(D, D), cross_wq=(D, 512), cross_wkv=(D, 1024), cross_wo=(512, D), ffn2_gate=(D, FF),
           ffn2_up=(D, FF), ffn2_down=(FF, D))


def _copy_h(P, hin, hT):
    k = P.k
    for c in range(KC):
        b = k.buf("hcp")
        k.dma(k.sp, hT[c * 128:(c + 1) * 128, :], hin[c * 128:(c + 1) * 128, :], writes=[P.dbuf["hT"]], tag=b)


def _get_prog(name):
    if name in _PROGS:
        return _PROGS[name]
    P = Prog(name)
    P.load_consts()
    if name == "X":
        x = P.din("x", [T, D])
        pos = P.din("pos", [T], I32)
        mem = P.din("mem", [256, D])
        hT = P.dout("hT", [D, T])
        ropeT = P.dout("ropeT", [4, 128, T])
        memT = P.dout("memT", [D, 256])
        phase_x(P, x, pos, hT, ropeT)
        phase_mem(P, mem, memT)
    elif name == "A":
        vec = P.din("vec", [128, NVC])
        P.load_vec(vec[:, :])
        hin = P.din("hTin", [D, T])
        ropeT = P.din("ropeT", [4, 128, T])
        W = {n: P.din(n, list(sh)) for n, sh in A_W.items()}
        hT = P.dout("hT", [D, T])
        QA = P.dout("QA", [512, T], BF16)
        QB = P.dout("QB", [1152, T], BF16)
        QC = P.dout("QC", [768, T], BF16)
        KF = P.dout("KF", [KF_ROWS, T], BF16)
        VT = P.dout("VT", [T, D], BF16)
        _copy_h(P, hin, hT)
        if "noffn" not in DBG:
            phase_ffn(P, hT, VC_FFN1, W["ffn1_gate"], W["ffn1_up"], W["ffn1_down"], TT=FFN_TT)
        if "noa2" not in DBG:
            phase_a2(P, hT, ropeT, W, QA, QB, QC, KF, VT)
    elif name == "B":
        vec = P.din("vec", [128, NVC])
        P.load_vec(vec[:, :])
        hin = P.din("hTin", [D, T])
        QA = P.din("QA", [512, T], BF16)
        QB = P.din("QB", [1152, T], BF16)
        QC = P.din("QC", [768, T], BF16)
        KFall = P.din("KFall", [2, KF_ROWS, T], BF16)
        VTall = P.din("VTall", [2, T, D], BF16)
        memT = P.din("memT", [D, 256])
        cmask = P.din("cmask", [NQT, 8, 128, 512])
        dmask = P.din("dmask", [NQT, 24, 128, 512])
        W = {n: P.din(n, list(sh)) for n, sh in B_W.items()}
        hT = P.dout("hT", [D, T])
        _copy_h(P, hin, hT)
        phase_b(P, hT, QA, QB, QC, KFall, VTall, memT, cmask, dmask, W)
        phase_ffn(P, hT, VC_FFN2, W["ffn2_gate"], W["ffn2_up"], W["ffn2_down"], TT=FFN_TT)
    elif name == "Z":
        vec = P.din("vec", [128, NVC])
        P.load_vec(vec[:, :])
        hin = P.din("hTin", [D, T])
        hT = P.dint("hT", [D, T])
        _copy_h(P, hin, hT)
        out = P.dout("out", [T, D])
        phase_z(P, hT, out, VC_FINAL)
    P.k.barrier()
    _PROGS[name] = P
    return P


_CONSTS = None


def _run(name, in_maps):
    global _CONSTS
    P = _get_prog(name)
    if _CONSTS is None:
        _CONSTS = _consts()
    maps = [dict(m, **_CONSTS) for m in in_maps]
    res = run_bass_kernel_spmd(P.nc, maps, core_ids=list(range(8)))
    return res.results


def _masks(par):
    cm = np.zeros((NQT, 8, 128, 512), np.float32)
    dm = np.zeros((NQT, 24, 128, 512), np.float32)
    kk = np.arange(128)[:, None]
    qq = np.arange(512)[None, :]
    for j in range(NQT):
        g = GT[par][j]
        for slot in range(2):
            for kb in range(4):
                kpos = (2 * j + slot) * 512 + kb * 128 + kk
                cm[j, slot * 4 + kb] = (kpos <= g * 512 + qq)
        lo, hi = DIL_TILES[j]
        for kt in range(lo, hi):
            for kb in range(4):
                dlt = (g * 512 + qq) - (kt * 512 + kb * 128 + kk)
                w = ((dlt >= 0) & (dlt <= 128)).astype(np.float32)
                w += ((dlt >= 0) & (dlt <= 512) & (dlt % 4 == 0))
                w += ((dlt >= 0) & (dlt <= 2048) & (dlt % 16 == 0))
                dm[j, (kt - lo) * 4 + kb] = w
    return cm, dm


def _layer_weights(inp, l):
    c = np.ascontiguousarray
    w_in = inp["w_in"][l]
    A = dict(ffn1_gate=inp["ffn1_gate"][l], ffn1_up=inp["ffn1_up"][l], ffn1_down=inp["ffn1_down"][l])
    A["w_r64"] = c(w_in[:, 0:1024])
    A["w_r128"] = c(w_in[:, 2624:4160])
    A["w_kr"] = c(w_in[:, 2560:2624])
    A["w_c"] = c(w_in[:, 1536:2560])
    A["w_v"] = c(np.concatenate([w_in[:, 1024:1536], w_in[:, 4160:4928]], axis=1))
    uq = inp["mla_w_uq"][l].reshape(512, 6, 192)
    A["w_uq"] = c(np.concatenate([uq[:, :, 0:128].reshape(512, 768), uq[:, :, 128:192].reshape(512, 384)], axis=1))
    ukv = inp["mla_w_ukv"][l].reshape(512, 6, 256)
    A["w_ukv"] = c(np.concatenate([ukv[:, :, 0:128].reshape(512, 768), ukv[:, :, 128:256].reshape(512, 768)],
                                  axis=1))
    B = dict(w_out=inp["w_out"][l], cross_wq=inp["cross_wq"][l], cross_wkv=inp["cross_wkv"][l],
             cross_wo=inp["cross_wo"][l], ffn2_gate=inp["ffn2_gate"][l], ffn2_up=inp["ffn2_up"][l],
             ffn2_down=inp["ffn2_down"][l])
    return A, B


def kernel(**inp):
    inp = {k_: np.asarray(v) for k_, v in inp.items()}
    x, mem, pos = inp["x"], inp["mem"], inp["positions"]
    cores = [(c // 2, c % 2) for c in range(8)]
    maps = [dict(x=_core_tokens(x, b, par), pos=_core_tokens(pos[..., None], b, par)[:, 0].astype(np.int32),
                 mem=np.ascontiguousarray(mem[b])) for (b, par) in cores]
    r = _run("X", maps)
    hT = [r[c]["hT"] for c in range(8)]
    ropeT = [r[c]["ropeT"] for c in range(8)]
    memT = [r[c]["memT"] for c in range(8)]
    masks = [_masks(0), _masks(1)]
    for l in range(L):
        vec = _vec_pack(inp, l)
        A, B = _layer_weights(inp, l)
        r = _run("A", [dict(A, vec=vec, hTin=hT[c], ropeT=ropeT[c]) for c in range(8)])
        hT = [r[c]["hT"] for c in range(8)]
        bmaps = []
        for c, (b, par) in enumerate(cores):
            c0 = 2 * b
            bmaps.append(dict(B, vec=vec, hTin=hT[c], QA=r[c]["QA"], QB=r[c]["QB"], QC=r[c]["QC"],
                              KFall=np.stack([r[c0]["KF"], r[c0 + 1]["KF"]]),
                              VTall=np.stack([r[c0]["VT"], r[c0 + 1]["VT"]]),
                              memT=memT[c], cmask=masks[par][0], dmask=masks[par][1]))
        r = _run("B", bmaps)
        hT = [r[c]["hT"] for c in range(8)]
    vec = _vec_pack(inp, 0)
    r = _run("Z", [dict(vec=vec, hTin=hT[c]) for c in range(8)])
    out = np.zeros((4, 4096, D), np.float32)
    for c, (b, par) in enumerate(cores):
        for j, g in enumerate(GT[par]):
            out[b, g * 512:(g + 1) * 512] = r[c]["out"][j * 512:(j + 1) * 512]
    return out


def proj_fm(P, x, xb, W, kcn, ncols, epi, ntok=512, banks=(0, 1)):
    k, nc = P.k, P.nc
    cpt = min(ncols, (WSLOT // kcn) // 128 * 128) if ncols >= 128 else ncols
    tiles = []
    c0 = 0
    while c0 < ncols:
        w_ = min(cpt, ncols - c0)
        P.wpush(W[:, c0:c0 + w_], kcn, w_)
        tiles.append((c0, w_))
        c0 += w_
    it = 0
    for (c0, w_) in tiles:
        wt, wb = P.wnext()
        for s0 in range(0, w_, 128):
            m = min(128, w_ - s0)
            pa, pb = P.bank(banks[it % len(banks)])
            it += 1
            for c in range(kcn):
                k.op(k.pe, lambda pa=pa, c=c, s0=s0, m=m, wt=wt: nc.tensor.matmul(
                    pa[0:m, 0:ntok], wt[:, c, s0:s0 + m], x[:, c, 0:ntok], start=(c == 0), stop=(c == kcn - 1)),
                    reads=[wb, xb], writes=[pb], signal=(c == kcn - 1))
            epi((c0 + s0) // 128, pa, pb, m)


def proj_tm(P, x, xb, W, kcn, ncols, epi, ntb=4, banks=(2, 3)):
    k, nc = P.k, P.nc
    cpt = min(ncols, (WSLOT // kcn) // 128 * 128, 512)
    tiles = []
    c0 = 0
    while c0 < ncols:
        w_ = min(cpt, ncols - c0)
        P.wpush(W[:, c0:c0 + w_], kcn, w_)
        tiles.append((c0, w_))
        c0 += w_
    it = 0
    for (c0, w_) in tiles:
        wt, wb = P.wnext()
        for tb in range(ntb):
            pa, pb = P.bank(banks[it % len(banks)])
            it += 1
            for c in range(kcn):
                k.op(k.pe, lambda pa=pa, c=c, tb=tb, wt=wt, w_=w_: nc.tensor.matmul(
                    pa[:, 0:w_], x[:, c, tb * 128:(tb + 1) * 128], wt[:, c, 0:w_], start=(c == 0),
                    stop=(c == kcn - 1)), reads=[wb, xb], writes=[pb], signal=(c == kcn - 1))
            epi(tb, c0, w_, pa, pb)


class Scr:
    def __init__(self, P, st, name, shape, dtype, n):
        self.t = st.enter_context(P.nc.sbuf_tensor(name, [128, n] + list(shape), dtype))
        self.b = [P.k.buf(name) for _ in range(n)]
        self.n = n
        self.i = 0

    def next(self):
        s = self.i % self.n
        self.i += 1
        return self.t[:, s], self.b[s]


def phase_a2(P, hT, ropeT, W, QA, QB, QC, KF, VT):
    k, nc = P.k, P.nc
    with ExitStack() as st:
        hS = st.enter_context(nc.sbuf_tensor("a_h", [128, KC, 512], F32))
        hb = [k.buf("ah") for _ in range(KC)]
        xn = st.enter_context(nc.sbuf_tensor("a_xn", [128, KC, 512], BF16))
        xnb = k.buf("axn")
        rt = st.enter_context(nc.sbuf_tensor("a_rt", [128, 4, 512], F32))
        rtb = k.buf("art")
        cq = st.enter_context(nc.sbuf_tensor("a_cq", [128, 8, 512], F32))
        cqb = [k.buf("acq") for _ in range(8)]
        cn = st.enter_context(nc.sbuf_tensor("a_cn", [128, 8, 512], BF16))
        cnb = [k.buf("acn"), k.buf("acn")]
        R = rms_alloc(P, st, 512, "a")
        ub = Scr(P, st, "a_ub", [512], BF16, 2)
        ub2 = Scr(P, st, "a_ub2", [512], BF16, 2)
        ob = Scr(P, st, "a_ob", [512], BF16, 4)
        for tt in range(NQT):
            t0 = tt * 512
            for c in range(KC):
                k.dma(k.sp, hS[:, c, :], hT[c * 128:(c + 1) * 128, t0:t0 + 512], reads=[P.dbuf["hT"]],
                      writes=[hb[c]], tag=hb[c])
            k.dma(k.sp, rt[:], ropeT[:, :, t0:t0 + 512].rearrange("f p t -> p f t"), reads=[P.dbuf["ropeT"]],
                  writes=[rtb], tag=rtb)
            rstd, rb = rms_rstd(P, R, lambda c: (hS[:, c, :], hb[c]), KC, D, 1e-6, [6])
            for c in range(KC):
                k.op(k.dve, lambda c=c: nc.vector.scalar_tensor_tensor(
                    out=xn[:, c, :], in0=hS[:, c, :], scalar=P.vec[:, VC_MIX + c:VC_MIX + c + 1], in1=rstd[:],
                    op0=ALU.mult, op1=ALU.mult), reads=[hb[c], rb, P.vbuf], writes=[xnb])

            def store(dst, dbn, src_ap, src_b):
                k.dma(k.sp, dst, src_ap, reads=[src_b], writes=[P.dbuf[dbn]], tag=src_b)

            def rope_epi(pa, pb, m, kind, dst, dbn):
                if "norope" in DBG:
                    return plain_epi(pa, pb, m, dst, dbn)
                perm = P.perm64 if kind == 64 else P.perm128
                ci_, si_ = (0, 1) if kind == 64 else (2, 3)
                wa, wb_ = ub.next()
                va, vb_ = ub2.next()
                k.op(k.dve, lambda: nc.vector.scalar_tensor_tensor(
                    out=wa[0:m, :], in0=pa[0:m, :], scalar=1.0, in1=rt[0:m, ci_, :], op0=ALU.mult, op1=ALU.mult),
                    reads=[pb, rtb], writes=[wb_])
                k.op(k.dve, lambda: nc.vector.scalar_tensor_tensor(
                    out=va[0:m, :], in0=pa[0:m, :], scalar=-1.0, in1=rt[0:m, si_, :], op0=ALU.mult, op1=ALU.mult),
                    reads=[pb, rtb], writes=[vb_])
                sa, sb = P.bank(4 + (ub.i % 2))
                k.op(k.pe, lambda: nc.tensor.matmul(sa[0:m, :], P.identb[0:m, 0:m], wa[0:m, :], start=True,
                                                    stop=False), reads=[wb_, P.cbuf], writes=[sb], signal=False)
                k.op(k.pe, lambda: nc.tensor.matmul(sa[0:m, :], perm[0:m, 0:m], va[0:m, :], start=False,
                                                    stop=True), reads=[vb_, P.cbuf], writes=[sb])
                oa, obb = ob.next()
                k.op(k.act, lambda: nc.scalar.copy(out=oa[0:m, :], in_=sa[0:m, :]), reads=[sb], writes=[obb])
                store(dst, dbn, oa[0:m, :], obb)

            def plain_epi(pa, pb, m, dst, dbn):
                oa, obb = ob.next()
                k.op(k.act, lambda: nc.scalar.copy(out=oa[0:m, :], in_=pa[0:m, :]), reads=[pb], writes=[obb])
                store(dst, dbn, oa[0:m, :], obb)

            def e64(ci, pa, pb, m):
                if ci < 4:
                    rope_epi(pa, pb, m, 64, QA[ci * 128:(ci + 1) * 128, t0:t0 + 512], "QA")
                else:
                    r0 = KOFF_A + (ci - 4) * 128
                    rope_epi(pa, pb, m, 64, KF[r0:r0 + 128, t0:t0 + 512], "KF")
            proj_fm(P, xn, xnb, W["w_r64"], KC, 1024, e64)

            def e128(ci, pa, pb, m):
                if ci < 6:
                    rope_epi(pa, pb, m, 128, QC[ci * 128:(ci + 1) * 128, t0:t0 + 512], "QC")
                else:
                    r0 = KOFF_C + (ci - 6) * 128
                    rope_epi(pa, pb, m, 128, KF[r0:r0 + 128, t0:t0 + 512], "KF")
            proj_fm(P, xn, xnb, W["w_r128"], KC, 1536, e128)

            def ekr(ci, pa, pb, m):
                rope_epi(pa, pb, 64, 64, KF[KOFF_BR:KOFF_BR + 64, t0:t0 + 512], "KF")
            if "nokr" not in DBG:
                proj_fm(P, xn, xnb, W["w_kr"], KC, 64, ekr)

            def ec(ci, pa, pb, m):
                k.op(k.act, lambda: nc.scalar.copy(out=cq[:, ci, :], in_=pa), reads=[pb], writes=[cqb[ci]])
            proj_fm(P, xn, xnb, W["w_c"], KC, 1024, ec)

            def ev(tb, c0, w_, pa, pb):
                oa, obb = ob.next()
                k.op(k.act, lambda: nc.scalar.copy(out=oa[:, 0:w_], in_=pa[:, 0:w_]), reads=[pb], writes=[obb])
                for s0 in range(c0, c0 + w_, 128):
                    dcol = VOFF_A + s0 if s0 < 512 else VOFF_C + (s0 - 512)
                    store(VT[t0 + tb * 128:t0 + (tb + 1) * 128, dcol:dcol + 128], "VT",
                          oa[:, s0 - c0:s0 - c0 + 128], obb)
            if "nov" not in DBG:
                proj_tm(P, xn, xnb, W["w_v"], KC, 1280, ev)
            if "nomla" in DBG:
                continue

            for half, gcol in ((0, VC_QN), (1, VC_KVN)):
                rs, rsb = rms_rstd(P, R, lambda c, half=half: (cq[:, half * 4 + c, :], cqb[half * 4 + c]), 4, 512,
                                   1e-6, [6])
                for c in range(4):
                    k.op(k.dve, lambda c=c, half=half, gcol=gcol: nc.vector.scalar_tensor_tensor(
                        out=cn[:, half * 4 + c, :], in0=cq[:, half * 4 + c, :],
                        scalar=P.vec[:, gcol + c:gcol + c + 1], in1=rs[:], op0=ALU.mult, op1=ALU.mult),
                        reads=[cqb[half * 4 + c], rsb, P.vbuf], writes=[cnb[half]])

            def euq(ci, pa, pb, m):
                if ci < 6:
                    plain_epi(pa, pb, m, QB[ci * 128:(ci + 1) * 128, t0:t0 + 512], "QB")
                else:
                    rope_epi(pa, pb, m, 64, QB[ci * 128:(ci + 1) * 128, t0:t0 + 512], "QB")
            proj_fm(P, cn[:, 0:4], cnb[0], W["w_uq"], 4, 1152, euq)

            def euk(ci, pa, pb, m):
                r0 = KOFF_BN + ci * 128
                plain_epi(pa, pb, m, KF[r0:r0 + 128, t0:t0 + 512], "KF")
            proj_fm(P, cn[:, 4:8], cnb[1], W["w_ukv"][:, 0:768], 4, 768, euk)

            def euv(tb, c0, w_, pa, pb):
                oa, obb = ob.next()
                k.op(k.act, lambda: nc.scalar.copy(out=oa[:, 0:w_], in_=pa[:, 0:w_]), reads=[pb], writes=[obb])
                store(VT[t0 + tb * 128:t0 + (tb + 1) * 128, VOFF_B + c0:VOFF_B + c0 + w_], "VT", oa[:, 0:w_], obb)
            proj_tm(P, cn[:, 4:8], cnb[1], W["w_ukv"][:, 768:1536], 4, 768, euv)
        k.barrier()


def phase_mem(P, mem_in, memT):
    k, nc = P.k, P.nc
    with ExitStack() as st:
        mi = st.enter_context(nc.sbuf_tensor("m_in", [128, 2, D], F32))
        mib = [k.buf("mi"), k.buf("mi")]
        mf = st.enter_context(nc.sbuf_tensor("m_f", [128, KC, 256], F32))
        mfb = [k.buf("mf") for _ in range(KC)]
        R = rms_alloc(P, st, 256, "m")
        for tb in range(2):
            k.dma(k.sp, mi[:, tb, :], mem_in[tb * 128:(tb + 1) * 128, :], writes=[mib[tb]], tag=mib[tb])
        for c in range(KC):
            pa, pb = P.bank(c % 4)
            for tb in range(2):
                k.op(k.pe, lambda pa=pa, c=c, tb=tb: nc.tensor.transpose(
                    pa[:, tb * 128:(tb + 1) * 128], mi[:, tb, c * 128:(c + 1) * 128], P.ident[:]),
                    reads=[mib[tb], P.cbuf], writes=[pb], signal=(tb == 1))
            k.op(k.dve, lambda pa=pa, c=c: nc.vector.tensor_copy(out=mf[:, c, :], in_=pa[:, 0:256]),
                 reads=[pb], writes=[mfb[c]])
        rstd, rb = rms_rstd(P, R, lambda c: (mf[:, c, :], mfb[c]), KC, D, 1e-6, [6])
        for c in range(KC):
            k.op(k.dve, lambda c=c: nc.vector.tensor_tensor(out=mf[:, c, :], in0=mf[:, c, :], in1=rstd[:],
                                                            op=ALU.mult), reads=[mfb[c], rb], writes=[mfb[c]])
            k.dma(k.sp, memT[c * 128:(c + 1) * 128, :], mf[:, c, :], reads=[mfb[c]], writes=[P.dbuf["memT"]],
                  tag=mfb[c])
        k.barrier()


def phase_b(P, hT, QA, QB, QC, KFall, VTall, memT, cmask, dmask, W):
    k, nc = P.k, P.nc
    hTb = P.dbuf["hT"]
    with ExitStack() as st:
        kmT = st.enter_context(nc.sbuf_tensor("b_kmT", [128, 4, 256], BF16))
        kmb = k.buf("kmT")
        vm = st.enter_context(nc.sbuf_tensor("b_vm", [128, 2, 512], BF16))
        vmb = k.buf("vm")
        lamt = st.enter_context(nc.sbuf_tensor("b_lam", [128, 8], F32))
        lamb = k.buf("lam")
        lsc = st.enter_context(nc.sbuf_tensor("b_lsc", [128, 2, 64], F32))
        lscb = k.buf("lsc")
        for i in range(2):
            k.op(k.dve, lambda i=i: nc.vector.tensor_tensor(
                out=lsc[:, i, :], in0=P.vec[:, VC_LAM + 128 * i:VC_LAM + 128 * i + 64],
                in1=P.vec[:, VC_LAM + 128 * i + 64:VC_LAM + 128 * i + 128], op=ALU.mult),
                reads=[P.vbuf, P.cbuf], writes=[lscb])
            k.op(k.dve, lambda i=i: nc.vector.reduce_sum(out=lamt[:, i:i + 1], in_=lsc[:, i, :],
                                                        axis=mybir.AxisListType.X), reads=[lscb], writes=[lamb])
        k.op(k.act, lambda: nc.scalar.activation(out=lamt[:, 2:4], in_=lamt[:, 0:2], func=AF.Exp),
             reads=[lamb], writes=[lamb])
        k.op(k.dve, lambda: nc.vector.tensor_tensor(out=lamt[:, 4:5], in0=lamt[:, 3:4], in1=lamt[:, 2:3],
                                                    op=ALU.subtract), reads=[lamb], writes=[lamb])
        k.op(k.dve, lambda: nc.vector.tensor_tensor(out=lamt[:, 5:6], in0=lamt[:, 4:5],
                                                    in1=P.vec[:, VC_LCST:VC_LCST + 1], op=ALU.add),
             reads=[lamb, P.vbuf], writes=[lamb])
        k.op(k.dve, lambda: nc.vector.tensor_tensor(out=lamt[:, 6:7], in0=P.vec[:, VC_SUBLN:VC_SUBLN + 1],
                                                    in1=P.vec[:, VC_LCST + 1:VC_LCST + 2], op=ALU.mult),
             reads=[lamb, P.vbuf], writes=[lamb])
        neglam = lamt[:, 5:6]
        subcol = lamt[:, 6:7]
        with ExitStack() as st2:
            mf = st2.enter_context(nc.sbuf_tensor("b_mf", [128, KC, 256], F32))
            mfb = k.buf("bmf")
            mn = st2.enter_context(nc.sbuf_tensor("b_mn", [128, KC, 256], BF16))
            mnb = k.buf("bmn")
            k.dma(k.sp, mf[:], memT.rearrange("(c p) t -> p c t", p=128), reads=[P.dbuf["memT"]], writes=[mfb],
                  tag=mfb)
            for c in range(KC):
                k.op(k.dve, lambda c=c: nc.vector.tensor_scalar(
                    out=mn[:, c, :], in0=mf[:, c, :], scalar1=P.vec[:, VC_MEM + c:VC_MEM + c + 1], scalar2=None,
                    op0=ALU.mult), reads=[mfb, P.vbuf], writes=[mnb])

            def ekm(ci, pa, pb, m):
                k.op(k.act, lambda: nc.scalar.copy(out=kmT[:, ci, :], in_=pa[:, 0:256]), reads=[pb], writes=[kmb])
            proj_fm(P, mn, mnb, W["cross_wkv"][:, 0:512], KC, 512, ekm, ntok=256)

            def evm(tb, c0, w_, pa, pb):
                k.op(k.act, lambda: nc.scalar.copy(out=vm[:, tb, c0:c0 + w_], in_=pa[:, 0:w_]), reads=[pb],
                     writes=[vmb])
            proj_tm(P, mn, mnb, W["cross_wkv"][:, 512:1024], KC, 512, evm, ntb=2)
            k.barrier()
        yT = st.enter_context(nc.sbuf_tensor("b_yT", [128, KC, 512], BF16))
        yb = [k.buf("yT") for _ in range(KC)]
        xn = st.enter_context(nc.sbuf_tensor("b_xn", [128, KC, 512], BF16))
        xnb = k.buf("bxn")
        qc = st.enter_context(nc.sbuf_tensor("b_qc", [128, 4, 512], BF16))
        qcb = [k.buf("qc") for _ in range(4)]
        oc = st.enter_context(nc.sbuf_tensor("b_oc", [128, 4, 512], BF16))
        ocb = k.buf("oc")
        cm = st.enter_context(nc.sbuf_tensor("b_cm", [128, 8, 512], BF16))
        cmb = k.buf("cm")
        dm = st.enter_context(nc.sbuf_tensor("b_dm", [128, 24, 512], BF16))
        dmb = k.buf("dm")
        qs = Scr(P, st, "b_q", [2, 512], BF16, 2)
        kts = Scr(P, st, "b_k", [2, 512], BF16, 3)
        vts = Scr(P, st, "b_v", [4, 128], BF16, 3)
        pts = Scr(P, st, "b_p", [512], BF16, 4)
        f1 = Scr(P, st, "b_f1", [512], F32, 3)
        f2 = Scr(P, st, "b_f2", [512], F32, 3)
        hs = Scr(P, st, "b_hs", [512], F32, 3)
        os_ = Scr(P, st, "b_os", [512], F32, 3)
        sqs = Scr(P, st, "b_sq", [512], BF16, 2)
        R = rms_alloc(P, st, 512, "br")
        cnt = {"s": 0, "ol": 0}

        def attn(qparts, kload, vload, nblk, maskfn, scale):
            oa, ob_ = P.bank(2 + 2 * (cnt["ol"] % 2))
            la, lb_ = P.bank(3 + 2 * (cnt["ol"] % 2))
            cnt["ol"] += 1
            for i in range(nblk):
                kaps, kbufs = kload(i)
                vap, vbuf = vload(i)
                sa, sb = P.bank(cnt["s"] % 2)
                cnt["s"] += 1
                for pi, (qa, qb_, m) in enumerate(qparts):
                    k.op(k.pe, lambda sa=sa, pi=pi, qa=qa, m=m, kaps=kaps: nc.tensor.matmul(
                        sa, kaps[pi], qa, start=(pi == 0), stop=(pi == len(qparts) - 1)),
                        reads=kbufs + [qb_], writes=[sb], signal=(pi == len(qparts) - 1))
                pa_, pb_ = pts.next()
                k.op(k.act, lambda sa=sa, pa_=pa_: nc.scalar.activation(out=pa_, in_=sa, func=AF.Exp, scale=scale),
                     reads=[sb], writes=[pb_])
                mk = maskfn(i)
                if mk is not None:
                    k.op(k.dve, lambda pa_=pa_, mk=mk: nc.vector.tensor_tensor(out=pa_, in0=pa_, in1=mk[0],
                                                                              op=ALU.mult),
                         reads=[pb_, mk[1], cmb, dmb], writes=[pb_])
                k.op(k.pe, lambda oa=oa, vap=vap, pa_=pa_, i=i: nc.tensor.matmul(
                    oa, vap, pa_, start=(i == 0), stop=(i == nblk - 1)), reads=[vbuf, pb_], writes=[ob_],
                    signal=(i == nblk - 1))
                k.op(k.pe, lambda la=la, pa_=pa_, i=i: nc.tensor.matmul(
                    la, P.ones[:], pa_, start=(i == 0), stop=(i == nblk - 1)), reads=[P.cbuf, pb_], writes=[lb_])
            return (oa, ob_), (la, lb_)

        def norm_out(O, Lx, dst, dstb):
            ra, rb_ = f1.next()
            k.op(k.dve, lambda: nc.vector.reciprocal(out=ra, in_=Lx[0]), reads=[Lx[1]], writes=[rb_])
            k.op(k.dve, lambda: nc.vector.scalar_tensor_tensor(out=dst, in0=O[0], scalar=1.0, in1=ra,
                                                               op0=ALU.mult, op1=ALU.mult),
                 reads=[O[1], rb_], writes=[dstb])

        def dram_kv(j, ktiles, kspec, vcol):
            state = {}

            def ensure(i):
                kt = ktiles[i // 4]
                if state.get("kt") != (i // 4):
                    r, lj = OWNER[kt]
                    ka, kb_ = kts.next()
                    for pi, (row0, m) in enumerate(kspec):
                        k.dma(k.sp, ka[0:m, pi, :], KFall[r, row0:row0 + m, lj * 512:(lj + 1) * 512],
                              reads=[P.dbuf["KFall"]], writes=[kb_], tag=kb_)
                    va, vb_ = vts.next()
                    k.dma(k.sp, va, VTall[r, lj * 512:(lj + 1) * 512, vcol:vcol + 128].rearrange(
                        "(b p) c -> p b c", p=128), reads=[P.dbuf["VTall"]], writes=[vb_], tag=vb_)
                    state.update(kt=i // 4, ka=ka, kb=kb_, va=va, vb=vb_)

            def kload(i):
                ensure(i)
                b = i % 4
                return [state["ka"][0:m, pi, b * 128:(b + 1) * 128] for pi, (row0, m) in enumerate(kspec)], \
                    [state["kb"]]

            def vload(i):
                ensure(i)
                return state["va"][:, i % 4, :], state["vb"]
            return kload, vload

        for j in range(NQT):
            t0 = j * 512
            k.dma(k.pool, cm[:], cmask[j].rearrange("u p q -> p u q"), writes=[cmb], tag=cmb)
            lo, hi = DIL_TILES[j]
            nd = (hi - lo) * 4
            k.dma(k.pool, dm[:, 0:nd, :], dmask[j, 0:nd].rearrange("u p q -> p u q"), writes=[dmb], tag=dmb)
            ctiles = list(range(0, 2 * j + 2))

            def cmaskfn(i, j=j):
                kt = i // 4
                if kt < 2 * j:
                    return None
                return cm[:, (kt - 2 * j) * 4 + i % 4, :], cmb

            def dmaskfn(i):
                return dm[:, i, :], dmb

            def qload(parts):
                qa, qb_ = qs.next()
                out = []
                for pi, (src, dbn, m) in enumerate(parts):
                    k.dma(k.sp, qa[0:m, pi, :], src, reads=[P.dbuf[dbn]], writes=[qb_], tag=qb_)
                    out.append((qa[0:m, pi, :], qb_, m))
                return out

            for h in range(4):
                res = []
                for br in range(2):
                    qrow = br * 256 + h * 64
                    qp = qload([(QA[qrow:qrow + 64, t0:t0 + 512], "QA", 64)])
                    kl, vl = dram_kv(j, ctiles, [(KOFF_A + br * 256 + h * 64, 64)], VOFF_A + h * 128)
                    res.append(attn(qp, kl, vl, len(ctiles) * 4, cmaskfn, 0.125))
                a1, a1b = f1.next()
                norm_out(res[0][0], res[0][1], a1, a1b)
                a2, a2b = f2.next()
                norm_out(res[1][0], res[1][1], a2, a2b)
                o_, ob2 = f2.next()
                k.op(k.dve, lambda: nc.vector.scalar_tensor_tensor(out=o_, in0=a2, scalar=neglam, in1=a1,
                                                                   op0=ALU.mult, op1=ALU.add),
                     reads=[a1b, a2b, lamb], writes=[ob2])
                sq, sqb_ = sqs.next()
                k.op(k.act, lambda: nc.scalar.activation(out=sq, in_=o_, func=AF.Square), reads=[ob2], writes=[sqb_])
                pa, pb = P.bank(6)
                k.op(k.pe, lambda: nc.tensor.matmul(pa, P.ones[:], sq, start=True, stop=True),
                     reads=[sqb_, P.cbuf], writes=[pb])
                rs, rsb = f1.next()
                k.op(k.act, lambda: nc.scalar.activation(out=rs, in_=pa, func=AF.Sqrt, bias=P.cst[:, 6:7],
                                                         scale=1.0 / 128), reads=[pb, P.cbuf], writes=[rsb])
                k.op(k.dve, lambda: nc.vector.reciprocal(out=rs, in_=rs), reads=[rsb], writes=[rsb])
                k.op(k.dve, lambda: nc.vector.scalar_tensor_tensor(out=yT[:, h, :], in0=o_, scalar=subcol, in1=rs,
                                                                   op0=ALU.mult, op1=ALU.mult),
                     reads=[ob2, rsb, lamb], writes=[yb[h]])
            for h in range(6):
                qp = qload([(QB[h * 128:(h + 1) * 128, t0:t0 + 512], "QB", 128),
                            (QB[768 + h * 64:768 + (h + 1) * 64, t0:t0 + 512], "QB", 64)])
                kl, vl = dram_kv(j, ctiles, [(KOFF_BN + h * 128, 128), (KOFF_BR, 64)], VOFF_B + h * 128)
                O, Lx = attn(qp, kl, vl, len(ctiles) * 4, cmaskfn, 192 ** -0.5)
                norm_out(O, Lx, yT[:, 4 + h, :], yb[4 + h])
            dtiles = list(range(lo, hi))
            for h in range(6):
                qp = qload([(QC[h * 128:(h + 1) * 128, t0:t0 + 512], "QC", 128)])
                kl, vl = dram_kv(j, dtiles, [(KOFF_C + h * 128, 128)], VOFF_C + h * 128)
                O, Lx = attn(qp, kl, vl, len(dtiles) * 4, dmaskfn, 128 ** -0.5)
                norm_out(O, Lx, yT[:, 10 + h, :], yb[10 + h])

            def res_epi(ci, pa, pb, m):
                ha, hb_ = hs.next()
                k.dma(k.sp, ha, hT[ci * 128:(ci + 1) * 128, t0:t0 + 512], reads=[hTb], writes=[hb_], tag=hb_)
                oa_, ob_ = os_.next()
                k.op(k.dve, lambda: nc.vector.scalar_tensor_tensor(out=oa_, in0=pa, scalar=1.0, in1=ha,
                                                                   op0=ALU.mult, op1=ALU.add),
                     reads=[pb, hb_], writes=[ob_])
                k.dma(k.sp, hT[ci * 128:(ci + 1) * 128, t0:t0 + 512], oa_, reads=[ob_], writes=[hTb], tag=ob_)
            _proj_multi(P, yT, yb, W["w_out"], KC, D, res_epi)

            def hload(c):
                ha, hb_ = hs.next()
                k.dma(k.sp, ha, hT[c * 128:(c + 1) * 128, t0:t0 + 512], reads=[hTb], writes=[hb_], tag=hb_)
                return ha, hb_
            rstd, rb = rms_rstd(P, R, hload, KC, D, 1e-6, [6])
            for c in range(KC):
                ha, hb_ = hload(c)
                k.op(k.dve, lambda c=c, ha=ha: nc.vector.scalar_tensor_tensor(
                    out=xn[:, c, :], in0=ha, scalar=P.vec[:, VC_CROSS + c:VC_CROSS + c + 1], in1=rstd[:],
                    op0=ALU.mult, op1=ALU.mult), reads=[hb_, rb, P.vbuf], writes=[xnb])

            def eq(ci, pa, pb, m):
                k.op(k.act, lambda: nc.scalar.copy(out=qc[:, ci, :], in_=pa), reads=[pb], writes=[qcb[ci]])
            proj_fm(P, xn, xnb, W["cross_wq"], KC, 512, eq, banks=(6, 7))
            for h in range(4):
                qp = [(qc[:, h, :], qcb[h], 128)]
                O, Lx = attn(qp, lambda i, h=h: ([kmT[:, h, i * 128:(i + 1) * 128]], [kmb]),
                             lambda i, h=h: (vm[:, i, h * 128:(h + 1) * 128], vmb), 2, lambda i: None, 128 ** -0.5)
                norm_out(O, Lx, oc[:, h, :], ocb)
            proj_fm(P, oc, ocb, W["cross_wo"], 4, D, res_epi, banks=(6, 7))
        k.barrier()


def _proj_multi(P, x, xbs, W, kcn, ncols, epi, banks=(6, 7)):
    k, nc = P.k, P.nc
    cpt = (WSLOT // kcn) // 128 * 128
    tiles = []
    for c0 in range(0, ncols, cpt):
        P.wpush(W[:, c0:c0 + cpt], kcn, cpt)
        tiles.append(c0)
    it = 0
    for c0 in tiles:
        wt, wb = P.wnext()
        for s0 in range(0, cpt, 128):
            pa, pb = P.bank(banks[it % len(banks)])
            it += 1
            for c in range(kcn):
                k.op(k.pe, lambda pa=pa, c=c, s0=s0, wt=wt: nc.tensor.matmul(
                    pa, wt[:, c, s0:s0 + 128], x[:, c, :], start=(c == 0), stop=(c == kcn - 1)),
                    reads=[wb, xbs[c]], writes=[pb], signal=(c == kcn - 1))
            epi((c0 + s0) // 128, pa, pb, 128)
```
